# Optimizing a Trainium2 kernel written in Bass

```python
import math
import jax, jax.numpy as jnp
from jax import lax
import numpy as np

D_MODEL = 4096
BATCH = 4
SEQ = 4096
DEPTH = 4

CHUNK = 64
N_MIXERS = 2
N_A = (DEPTH + 1) // 2
N_B = DEPTH // 2
GMLP_BLOCK = 128
GMLP_WIDTH = D_MODEL
GMLP_GROUPS = 32
GMLP_GROUP_DIM = GMLP_WIDTH // GMLP_GROUPS
SB_HEADS = 32
SB_HEAD_DIM = D_MODEL // SB_HEADS
SB_QBLOCK = 128
FFN_HIDDEN = int(math.ceil(8 * D_MODEL / 3 / 256)) * 256
RMS_EPS = 1e-6

kernel_name = "hybrid_gmlp_stickbreaking_trunk"


def rms_norm(x, g):
    xf = x.astype(jnp.float32)
    y = xf * lax.rsqrt(jnp.mean(xf * xf, axis=-1, keepdims=True) + RMS_EPS)
    return (y * g.astype(jnp.float32)).astype(x.dtype)


def gmlp_mixer(h, w_in, v_norm, w_s, b_s, w_out):
    B, S, _ = h.shape
    z = jax.nn.gelu(h @ w_in, approximate=False)
    u, v = z[..., :GMLP_WIDTH], z[..., GMLP_WIDTH:]
    v = rms_norm(v, v_norm)
    n_blk = S // GMLP_BLOCK
    vr = v.reshape(B, n_blk, GMLP_BLOCK, GMLP_GROUPS, GMLP_GROUP_DIM)
    pos = jnp.arange(GMLP_BLOCK)
    mask = (pos[None, :] // CHUNK) <= (pos[:, None] // CHUNK)
    w = jnp.where(mask[None], w_s, jnp.zeros_like(w_s))
    sv = jnp.einsum('gts,bnsgc->bntgc', w.astype(vr.dtype), vr)
    sv = sv + b_s.T.astype(sv.dtype)[None, None, :, :, None]
    gated = u * sv.reshape(B, S, GMLP_WIDTH)
    return gated @ w_out


def stick_breaking_attention(q, k, v):
    S = q.shape[2]
    scale = 1.0 / math.sqrt(SB_HEAD_DIM)
    outs = []
    for blk in range(S // SB_QBLOCK):
        q0 = blk * SB_QBLOCK
        kl = q0 + SB_QBLOCK
        qb = q[:, :, q0:kl].astype(jnp.float32)
        kb = k[:, :, :kl].astype(jnp.float32)
        vb = v[:, :, :kl]
        z = jnp.einsum('bhtd,bhsd->bhts', qb, kb) * scale
        t_idx = q0 + jnp.arange(SB_QBLOCK)[:, None]
        s_idx = jnp.arange(kl)[None, :]
        strict = s_idx < t_idx
        log_keep = jnp.where(strict, jax.nn.log_sigmoid(-z), 0.0)
        suffix = lax.cumsum(log_keep, axis=3, reverse=True) - log_keep
        log_a = jax.nn.log_sigmoid(z) + suffix
        a = jnp.where(strict, jnp.exp(log_a), 0.0)
        outs.append(jnp.einsum('bhts,bhsd->bhtd', a.astype(vb.dtype), vb))
    return jnp.concatenate(outs, axis=2)


def sb_mixer(h, w_qkv, q_norm, k_norm, w_out):
    B, S, _ = h.shape
    qkv = (h @ w_qkv).reshape(B, S, 3, SB_HEADS, SB_HEAD_DIM)
    q = rms_norm(qkv[:, :, 0], q_norm)
    k = rms_norm(qkv[:, :, 1], k_norm)
    v = qkv[:, :, 2]
    q, k, v = (t.transpose(0, 2, 1, 3) for t in (q, k, v))
    o = stick_breaking_attention(q, k, v)
    o = o.transpose(0, 2, 1, 3).reshape(B, S, SB_HEADS * SB_HEAD_DIM)
    return o @ w_out


def swiglu_ffn(h, w_gate, w_up, w_down):
    return (jax.nn.silu(h @ w_gate) * (h @ w_up)) @ w_down


def setup_inputs(seed: int = 0) -> dict:
    key = jax.random.key(seed)
    ks = jax.random.split(key, 17)

    def nrm(k, shape, scale):
        return jax.random.normal(k, shape, jnp.float32) * scale

    def gain(k, shape):
        return 1.0 + 0.02 * jax.random.normal(k, shape, jnp.float32)

    D, F = D_MODEL, FFN_HIDDEN
    return {
        "x": nrm(ks[0], (BATCH, SEQ, D), 1.0),
        "a_norm": gain(ks[1], (N_A, D)),
        "a_w_in": nrm(ks[2], (N_A, D, 2 * GMLP_WIDTH), D ** -0.5),
        "a_v_norm": gain(ks[3], (N_A, GMLP_WIDTH)),
        "a_w_s": nrm(ks[4], (N_A, GMLP_GROUPS, GMLP_BLOCK, GMLP_BLOCK), GMLP_BLOCK ** -0.5),
        "a_b_s": 1.0 + 0.02 * jax.random.normal(ks[5], (N_A, GMLP_GROUPS, GMLP_BLOCK), jnp.float32),
        "a_w_out": nrm(ks[6], (N_A, GMLP_WIDTH, D), GMLP_WIDTH ** -0.5),
        "b_norm": gain(ks[7], (N_B, D)),
        "b_w_qkv": nrm(ks[8], (N_B, D, 3 * SB_HEADS * SB_HEAD_DIM), D ** -0.5),
        "b_q_norm": gain(ks[9], (N_B, SB_HEAD_DIM)),
        "b_k_norm": gain(ks[10], (N_B, SB_HEAD_DIM)),
        "b_w_out": nrm(ks[11], (N_B, SB_HEADS * SB_HEAD_DIM, D), (SB_HEADS * SB_HEAD_DIM) ** -0.5),
        "f_norm": gain(ks[12], (DEPTH, D)),
        "f_w_gate": nrm(ks[13], (DEPTH, D, F), D ** -0.5),
        "f_w_up": nrm(ks[14], (DEPTH, D, F), D ** -0.5),
        "f_w_down": nrm(ks[15], (DEPTH, F, D), F ** -0.5),
    }


def reference(x, a_norm, a_w_in, a_v_norm, a_w_s, a_b_s, a_w_out,
              b_norm, b_w_qkv, b_q_norm, b_k_norm, b_w_out,
              f_norm, f_w_gate, f_w_up, f_w_down):
    for i in range(DEPTH):
        j = i // N_MIXERS
        if i % N_MIXERS == 0:
            h = rms_norm(x, a_norm[j])
            x = x + gmlp_mixer(h, a_w_in[j], a_v_norm[j], a_w_s[j], a_b_s[j], a_w_out[j])
        else:
            h = rms_norm(x, b_norm[j])
            x = x + sb_mixer(h, b_w_qkv[j], b_q_norm[j], b_k_norm[j], b_w_out[j])
        h = rms_norm(x, f_norm[i])
        x = x + swiglu_ffn(h, f_w_gate[i], f_w_up[i], f_w_down[i])
    return x
```

```python
import math
from contextlib import ExitStack

import numpy as np
import ml_dtypes

import concourse.bass as bass
import concourse.mybir as mybir
from concourse.bass_utils import run_bass_kernel_spmd

F32 = mybir.dt.float32
BF16 = mybir.dt.bfloat16
AF = mybir.ActivationFunctionType
ALU = mybir.AluOpType

D = 4096
DC = 32
SEQ = 4096
BATCH = 4
T = 2048
NBLK = 16
F = 11008
FC = 86
FH = 43
DEPTH = 4
EPS = 1e-6
NCORES = 8
NEG = -30000.0


class PhysSem:
    def __init__(self, h):
        self.h = h
        self.count = 0


class Sem:
    def __init__(self, phys, gen):
        self.phys = phys
        self.h = phys.h
        self.base = phys.count
        self.n = 0
        self.gen = gen


class Prog:
    def __init__(self, nc, stack):
        self.nc = nc
        self.stack = stack
        self.q = {e: [] for e in ("sync", "scalar", "vector", "gpsimd", "tensor")}
        self.nsem = 0
        self.gen = 0
        self.free = []
        self.live = []

    def sem(self, name):
        if self.free:
            phys = self.free.pop()
        else:
            self.nsem += 1
            phys = PhysSem(self.stack.enter_context(self.nc.semaphore(f"sem_{self.nsem}")))
        v = Sem(phys, self.gen)
        self.live.append(v)
        return v

    def next_gen(self):
        self.gen += 1
        keep = []
        for v in self.live:
            if v.gen <= self.gen - 2:
                v.phys.count = v.base + v.n
                self.free.append(v.phys)
            else:
                keep.append(v)
        self.live = keep

    def sb(self, name, shape, dt):
        return self.stack.enter_context(self.nc.sbuf_tensor(name, shape, dt))

    def psum(self, name, shape, dt):
        return self.stack.enter_context(self.nc.psum_tensor(name, shape, dt))

    def op(self, eng, fn, inc=None):
        if inc is None:
            self.q[eng].append(fn)
            return None
        step = 16 if getattr(fn, "_is_dma", False) else 1
        inc.n += step
        h = inc.h

        def f(e, fn=fn, h=h, step=step):
            fn(e).then_inc(h, step)
        self.q[eng].append(f)
        return inc.n

    def dma(self, eng, out, in_, inc=None):
        def fn(e, out=out, in_=in_):
            return e.dma_start(out=out, in_=in_)
        fn._is_dma = True
        return self.op(eng, fn, inc)

    def wait(self, eng, sem, val):
        if val is None or val <= 0:
            return
        h = sem.h
        val = val + sem.base
        self.q[eng].append(lambda e, h=h, val=val: e.wait_ge(h, val))

    def emit(self):
        nc = self.nc
        with nc.Block() as block:
            @block.sync
            def _(e):
                for f in self.q["sync"]:
                    f(e)

            @block.scalar
            def _(e):
                for f in self.q["scalar"]:
                    f(e)

            @block.vector
            def _(e):
                for f in self.q["vector"]:
                    f(e)

            @block.gpsimd
            def _(e):
                for f in self.q["gpsimd"]:
                    f(e)

            @block.tensor
            def _(e):
                for f in self.q["tensor"]:
                    f(e)


class Ctx:
    pass


GCOL = {"a_norm": (0, 32), "a_vn": (64, 96), "b_norm": (128, 160), "f_norm": (192, 224, 256, 288),
        "q": (320, 321), "k": (322, 323)}
NGAIN = 324


def setup_ctx(P, gains_ap, ngain, consts_ap=None):
    nc = P.nc
    C = Ctx()
    C.R = P.sb("R", [128, 65536], BF16)
    C.WA = P.sb("WA", [128, 16384], BF16)
    C.FA = P.sb("FA", [128, 8192], F32)
    C.BA = P.sb("BA", [128, 4096], BF16)
    C.gains = P.sb("gains_sb", [128, ngain], F32)
    C.ones = P.sb("ones", [128, 128], BF16)
    C.ones128 = P.sb("ones128", [128, 128], BF16)
    C.rs = P.sb("rs", [128, 2, 128], F32)
    C.epsb = P.sb("epsb", [128, 1], F32)
    C.PS = [P.psum(f"ps{i}", [128, 512], F32) for i in range(8)]
    C.init = P.sem("init")
    P.op("vector", lambda e: e.memset(C.ones[:], 1.0 / D))
    P.op("vector", lambda e: e.memset(C.ones128[:], 1.0 / 128))
    P.op("vector", lambda e: e.memset(C.epsb[:], EPS), inc=C.init)
    v = P.dma("sync", C.gains[:], gains_ap, inc=C.init)
    if consts_ap is not None:
        C.cst = P.sb("cst", [128, 3, 128], BF16)
        C.ident = C.cst[:, 0, :]
        C.negtri = C.cst[:, 1, :]
        C.negones = C.cst[:, 2, :]
        C.ssqp = P.sb("ssqp", [128, 16, 16], F32)
        P.dma("gpsimd", C.cst[:], consts_ap, inc=C.init)
        P.wait("vector", C.init, C.init.n)
        for col in GCOL["q"]:
            P.op("vector", lambda e, col=col: e.tensor_scalar(
                out=C.gains[:, col:col + 1], in0=C.gains[:, col:col + 1], scalar1=1.0 / math.sqrt(128.0),
                scalar2=None, op0=ALU.mult), inc=C.init)
    C.init_val = C.init.n
    for eng in ("scalar", "vector", "tensor", "gpsimd"):
        P.wait(eng, C.init, C.init_val)
    return C


def barrier(P, dones):
    for eng in ("sync", "scalar", "vector", "gpsimd", "tensor"):
        for (s, v) in dones:
            P.wait(eng, s, v)
    P.next_gen()


def phase_norm(P, C, xsrc, gcol, tag):
    hT = C.R[:, :].rearrange("p (c t) -> p c t", c=DC)
    xv = xsrc.rearrange("c p t -> p c t")
    xt = [C.FA[:, 0:4096].rearrange("p (c t) -> p c t", c=DC),
          C.FA[:, 4096:8192].rearrange("p (c t) -> p c t", c=DC)]
    sq = C.BA[:, 0:4096].rearrange("p (c t) -> p c t", c=DC)
    s_ld, s_sq, s_ss, s_rt, s_dv = (P.sem(tag + n) for n in ("ld", "sq", "ss", "rt", "dv"))
    g_b = C.gains[:, gcol:gcol + DC].unsqueeze(2).broadcast_to([128, DC, 128])
    for i in range(NBLK):
        b = i % 2
        tok = slice(i * 128, (i + 1) * 128)
        P.wait("sync", s_dv, 3 * (i - 1))
        P.dma("sync", xt[b], xv[:, :, tok], inc=s_ld)
        P.wait("scalar", s_ld, 16 * (i + 1))
        P.wait("scalar", s_ss, i)
        P.op("scalar", lambda e, b=b: e.activation(out=sq, in_=xt[b], func=AF.Square), inc=s_sq)
        P.wait("tensor", s_sq, i + 1)
        P.wait("tensor", s_rt, i - 1)
        ps = C.PS[b][:, 0:128]
        for c in range(DC):
            fn = lambda e, c=c, ps=ps: e.matmul(ps, lhsT=C.ones[:], rhs=sq[:, c, :],
                                                start=(c == 0), stop=(c == DC - 1))
            P.op("tensor", fn, inc=(s_ss if c == DC - 1 else None))
        P.wait("scalar", s_ss, i + 1)
        P.wait("scalar", s_dv, 3 * (i - 1))
        P.op("scalar", lambda e, b=b, ps=ps: e.activation(out=C.rs[:, b, :], in_=ps, func=AF.Sqrt,
                                                          bias=C.epsb[:], scale=1.0), inc=s_rt)
        P.wait("vector", s_rt, i + 1)
        P.op("vector", lambda e, b=b: e.reciprocal(out=C.rs[:, b, :], in_=C.rs[:, b, :]), inc=s_dv)
        P.wait("vector", s_dv, 3 * i + 1)
        rb = C.rs[:, b, :].unsqueeze(1).broadcast_to([128, DC, 128])
        P.op("vector", lambda e, b=b, rb=rb: e.tensor_tensor(out=xt[b], in0=xt[b], in1=rb, op=ALU.mult),
             inc=s_dv)
        P.wait("vector", s_dv, 3 * i + 2)
        P.op("vector", lambda e, b=b, tok=tok: e.tensor_tensor(out=hT[:, :, tok], in0=xt[b], in1=g_b,
                                                               op=ALU.mult), inc=s_dv)
    return (s_dv, 3 * NBLK)


def phase_ffn_g(P, C, wgu, actT, tag):
    hT = C.R[:, :].rearrange("p (c t) -> p c t", c=DC)
    wb = [C.WA[:, 0:8192], C.WA[:, 8192:16384]]
    sg = [C.FA[:, 0:512], C.FA[:, 512:1024]]
    ab = [C.BA[:, 0:2048], C.BA[:, 2048:4096]]
    s_w, s_g, s_u, s_a, s_d, s_st = (P.sem(tag + n) for n in ("w", "g", "u", "a", "d", "st"))
    for fc in range(FC):
        wbuf = wb[fc % 2]
        P.wait("gpsimd", s_u, 4 * (fc - 1))
        P.dma("gpsimd", wbuf, wgu[fc], inc=s_w)
        wv = wbuf.rearrange("p (s c j) -> p s c j", s=2, c=DC)
        for tt in range(4):
            gi = fc * 4 + tt
            pg = C.PS[(gi % 4) * 2]
            pu = C.PS[(gi % 4) * 2 + 1]
            tok = slice(tt * 512, (tt + 1) * 512)
            if tt == 0:
                P.wait("tensor", s_w, 16 * (fc + 1))
            P.wait("tensor", s_d, gi - 3)
            for s, pp, sem in ((0, pg, s_g), (1, pu, s_u)):
                for c in range(DC):
                    fn = lambda e, s=s, c=c, pp=pp, wv=wv, tok=tok: e.matmul(
                        pp[:], lhsT=wv[:, s, c, :], rhs=hT[:, c, tok], start=(c == 0), stop=(c == DC - 1))
                    P.op("tensor", fn, inc=(sem if c == DC - 1 else None))
            P.wait("scalar", s_g, gi + 1)
            P.wait("scalar", s_d, gi - 1)
            P.op("scalar", lambda e, gi=gi, pg=pg: e.activation(out=sg[gi % 2], in_=pg[:], func=AF.Silu),
                 inc=s_a)
            P.wait("vector", s_a, gi + 1)
            P.wait("vector", s_u, gi + 1)
            if tt == 0:
                P.wait("vector", s_st, 16 * (fc - 1))
            P.op("vector", lambda e, gi=gi, pu=pu, fc=fc, tok=tok: e.tensor_tensor(
                out=ab[fc % 2][:, tok], in0=sg[gi % 2], in1=pu[:], op=ALU.mult), inc=s_d)
        P.wait("sync", s_d, 4 * (fc + 1))
        P.dma("sync", actT[fc], ab[fc % 2], inc=s_st)
    return (s_st, 16 * FC)


def phase_proj_res(P, C, wd, KC, NTH, NFH, actT, xsrc, xdst, tag):
    TH = T // NTH
    NT2 = TH // 512
    Rv = C.R[:, 0:KC * TH].rearrange("p (c t) -> p c t", c=KC)
    wb = [C.WA[:, 0:KC * 128], C.WA[:, 8192:8192 + KC * 128]]
    xs = [C.FA[:, k * 512:(k + 1) * 512] for k in range(4)]
    s_al, s_w, s_m, s_xl, s_dv, s_st = (P.sem(tag + n) for n in ("al", "w", "m", "xl", "dv", "st"))
    groups = []
    for th in range(NTH):
        for fh in range(NFH):
            for dc in range(DC):
                for t2 in range(NT2):
                    groups.append((th, fh, dc, t2))
    NG = len(groups)
    store_val = {}

    def issue_xload(n):
        th, fh, dc, t2 = groups[n]
        tok = slice(th * TH + t2 * 512, th * TH + (t2 + 1) * 512)
        P.wait("sync", s_st, 16 * (n - 3))
        if fh >= 1:
            P.wait("sync", s_st, store_val[(th, fh - 1, dc, t2)])
        src = xsrc if fh == 0 else xdst
        P.dma("sync", xs[n % 4], src[dc, :, tok], inc=s_xl)

    n = 0
    nw = 0
    for th in range(NTH):
        for fh in range(NFH):
            p = th * NFH + fh
            al_val = 0
            if actT is not None:
                av = actT.rearrange("c p t -> p c t")
                P.wait("sync", s_m, n)
                NSPL = 4
                for k in range(NSPL):
                    c0, c1 = (KC * k) // NSPL, (KC * (k + 1)) // NSPL
                    P.dma("sync", Rv[:, c0:c1, :],
                          av[:, fh * KC + c0:fh * KC + c1, th * TH:(th + 1) * TH], inc=s_al)
                al_val = s_al.n
            if p == 0:
                for k in range(3):
                    issue_xload(k)
            for dc in range(DC):
                wbuf = wb[nw % 2]
                P.wait("gpsimd", s_m, NT2 * (nw - 1))
                P.dma("gpsimd", wbuf, wd[fh, dc], inc=s_w)
                wv = wbuf.rearrange("p (c j) -> p c j", c=KC)
                for t2 in range(NT2):
                    ps = C.PS[n % 8]
                    tok = slice(t2 * 512, (t2 + 1) * 512)
                    if t2 == 0:
                        P.wait("tensor", s_w, 16 * (nw + 1))
                        if dc == 0:
                            P.wait("tensor", s_al, al_val)
                    P.wait("tensor", s_dv, n - 7)
                    for k in range(KC):
                        fn = lambda e, k=k, ps=ps, wv=wv, tok=tok: e.matmul(
                            ps[:], lhsT=wv[:, k, :], rhs=Rv[:, k, tok], start=(k == 0), stop=(k == KC - 1))
                        P.op("tensor", fn, inc=(s_m if k == KC - 1 else None))
                    P.wait("vector", s_m, n + 1)
                    P.wait("vector", s_xl, 16 * (n + 1))
                    P.op("vector", lambda e, n=n, ps=ps: e.tensor_tensor(
                        out=xs[n % 4], in0=xs[n % 4], in1=ps[:], op=ALU.add), inc=s_dv)
                    if n + 3 < NG:
                        issue_xload(n + 3)
                    P.wait("sync", s_dv, n + 1)
                    tokg = slice(th * TH + t2 * 512, th * TH + (t2 + 1) * 512)
                    store_val[(th, fh, dc, t2)] = P.dma("sync", xdst[dc, :, tokg], xs[n % 4], inc=s_st)
                    n += 1
                nw += 1
    return (s_st, s_st.n)


def phase_tm(P, C, w, vout, mode, ssqp, tag):
    hT = C.R[:, :].rearrange("p (c t) -> p c t", c=DC)
    wb = [C.WA[:, 0:8192], C.WA[:, 8192:16384]]
    vb = [C.BA[:, k * 256:(k + 1) * 256] for k in range(4)]
    junk = C.FA[:, 0:256]
    vov = vout.rearrange("g p t c -> p g t c")
    s_w, s_m, s_ev, s_q, s_st = (P.sem(tag + n) for n in ("w", "m", "ev", "q", "st"))
    for fb in range(16):
        wbuf = wb[fb % 2]
        P.wait("gpsimd", s_m, 16 * (fb - 1))
        P.dma("gpsimd", wbuf, w[fb], inc=s_w)
        wv = wbuf.rearrange("p (c j) -> p c j", c=DC)
        for tb in range(NBLK):
            n = fb * 16 + tb
            ps = C.PS[n % 8][:, 0:256]
            if tb == 0:
                P.wait("tensor", s_w, 16 * (fb + 1))
            P.wait("tensor", s_ev, n - 7)
            for c in range(DC):
                fn = lambda e, c=c, ps=ps, wv=wv, tb=tb: e.matmul(
                    ps, lhsT=hT[:, c, tb * 128:(tb + 1) * 128], rhs=wv[:, c, :],
                    start=(c == 0), stop=(c == DC - 1))
                P.op("tensor", fn, inc=(s_m if c == DC - 1 else None))
            P.wait("scalar", s_m, n + 1)
            P.wait("scalar", s_st, 16 * (n - 3))
            if mode == "gelu":
                P.op("scalar", lambda e, n=n, ps=ps: e.activation(out=vb[n % 4], in_=ps, func=AF.Gelu),
                     inc=s_ev)
                P.wait("scalar", s_ev, n + 1)
                P.op("scalar", lambda e, n=n, tb=tb, fb=fb: e.activation(
                    out=junk, in_=vb[n % 4], func=AF.Square, accum_out=ssqp[:, tb, fb:fb + 1]), inc=s_q)
                P.wait("sync", s_q, n + 1)
            else:
                P.op("scalar", lambda e, n=n, ps=ps: e.activation(out=vb[n % 4], in_=ps, func=AF.Copy),
                     inc=s_ev)
                P.wait("sync", s_ev, n + 1)
            P.dma("sync", vov[:, 2 * fb:2 * fb + 2, tb, :],
                  vb[n % 4].rearrange("p (g c) -> p g c", g=2), inc=s_st)
    return (s_st, s_st.n)


def phase_gmlp_gate(P, C, wu, wsT, bs, vtm, gT, gvcol, ssqp, tag):
    hT = C.R[:, :].rearrange("p (c t) -> p c t", c=DC)
    ub = [C.WA[:, 0:4096], C.WA[:, 4096:8192]]
    vn = [C.WA[:, 8192:10240].rearrange("p (t c) -> p t c", t=NBLK),
          C.WA[:, 10240:12288].rearrange("p (t c) -> p t c", t=NBLK)]
    gb = [C.WA[:, 12288:14336], C.WA[:, 14336:16384]]
    wsb = C.BA[:, 0:4096].rearrange("p (g t) -> p g t", g=32)
    bsb = C.FA[:, 0:4096].rearrange("p (g t) -> p g t", g=32)
    ug = [C.FA[:, 4096:4608], C.FA[:, 4608:5120]]
    sv = [C.FA[:, 5120:5632], C.FA[:, 5632:6144]]
    rstd = C.rs[:, 0, 0:16]
    s_i, s_w, s_vl, s_vn, s_sv, s_u, s_a, s_d1, s_ev, s_st = (
        P.sem(tag + n) for n in ("i", "w", "vl", "vn", "sv", "u", "a", "d1", "ev", "st"))
    P.dma("gpsimd", C.BA[:, 0:4096], wsT, inc=s_i)
    P.dma("sync", bsb, bass.AP(bs.tensor, 0, [[0, 128], [128, 32], [1, 128]]), inc=s_i)
    P.wait("vector", s_i, 32)
    P.op("vector", lambda e: e.memset(wsb[64:128, :, 0:64], 0.0), inc=s_i)
    P.op("vector", lambda e: e.tensor_reduce(out=rstd, in_=ssqp[:, :, :], axis=mybir.AxisListType.X,
                                             op=ALU.add), inc=s_i)
    P.wait("scalar", s_i, 34)
    P.op("scalar", lambda e: e.activation(out=rstd, in_=rstd, func=AF.Sqrt, bias=C.epsb[:], scale=1.0 / D),
         inc=s_i)
    P.wait("vector", s_i, 35)
    P.op("vector", lambda e: e.reciprocal(out=rstd, in_=rstd), inc=s_i)
    P.wait("vector", s_i, 36)
    P.wait("tensor", s_i, 36)
    rb = rstd.unsqueeze(2).broadcast_to([128, NBLK, 128])
    for g in range(32):
        P.wait("gpsimd", s_u, 4 * (g - 1))
        P.dma("gpsimd", ub[g % 2], wu[g], inc=s_w)
        uv = ub[g % 2].rearrange("p (c j) -> p c j", c=DC)
        P.wait("sync", s_sv, 4 * (g - 1))
        P.dma("sync", vn[g % 2], vtm[g], inc=s_vl)
        P.wait("vector", s_vl, 16 * (g + 1))
        P.op("vector", lambda e, g=g: e.tensor_tensor(out=vn[g % 2], in0=vn[g % 2], in1=rb, op=ALU.mult),
             inc=s_vn)
        for tt in range(4):
            n = g * 4 + tt
            psv = C.PS[(n % 4) * 2]
            pu = C.PS[(n % 4) * 2 + 1]
            tok = slice(tt * 512, (tt + 1) * 512)
            if tt == 0:
                P.wait("tensor", s_vn, g + 1)
                P.wait("tensor", s_w, 16 * (g + 1))
            P.wait("tensor", s_ev, n - 3)
            for k in range(4):
                tb = tt * 4 + k
                fn = lambda e, k=k, tb=tb, g=g, psv=psv: e.matmul(
                    psv[:, k * 128:(k + 1) * 128], lhsT=vn[g % 2][:, tb, :], rhs=wsb[:, g, :],
                    start=True, stop=True)
                P.op("tensor", fn, inc=(s_sv if k == 3 else None))
            for c in range(DC):
                fn = lambda e, c=c, pu=pu, uv=uv, tok=tok: e.matmul(
                    pu[:], lhsT=uv[:, c, :], rhs=hT[:, c, tok], start=(c == 0), stop=(c == DC - 1))
                P.op("tensor", fn, inc=(s_u if c == DC - 1 else None))
            P.wait("scalar", s_u, n + 1)
            P.wait("scalar", s_ev, n - 1)
            P.op("scalar", lambda e, n=n, pu=pu: e.activation(out=ug[n % 2], in_=pu[:], func=AF.Gelu),
                 inc=s_a)
            P.wait("vector", s_sv, n + 1)
            bb = bsb[:, g, :].unsqueeze(1).broadcast_to([128, 4, 128])
            P.op("vector", lambda e, n=n, g=g, psv=psv, bb=bb: e.scalar_tensor_tensor(
                out=sv[n % 2].rearrange("p (k t) -> p k t", k=4),
                in0=psv[:, :].rearrange("p (k t) -> p k t", k=4),
                scalar=C.gains[:, gvcol + g:gvcol + g + 1], in1=bb, op0=ALU.mult, op1=ALU.add), inc=s_d1)
            P.wait("vector", s_d1, n + 1)
            P.wait("vector", s_a, n + 1)
            if tt == 0:
                P.wait("vector", s_st, 16 * (g - 1))
            P.op("vector", lambda e, n=n, g=g, tok=tok: e.tensor_tensor(
                out=gb[g % 2][:, tok], in0=ug[n % 2], in1=sv[n % 2], op=ALU.mult), inc=s_ev)
        P.wait("sync", s_ev, 4 * (g + 1))
        P.dma("sync", gT[g], gb[g % 2], inc=s_st)
    return (s_st, s_st.n)


def phase_qk(P, C, wqk, qT, kT, gq_col, gk_col, tag):
    hT = C.R[:, :].rearrange("p (c t) -> p c t", c=DC)
    wb = [C.WA[:, 0:4096], C.WA[:, 4096:8192]]
    ob = [C.WA[:, 8192:10240], C.WA[:, 10240:12288]]
    sq = [C.BA[:, 0:512], C.BA[:, 512:1024]]
    rt = [C.FA[:, 0:512], C.FA[:, 512:1024]]
    s_w, s_m, s_sq, s_o, s_rt, s_rc, s_ev, s_st = (
        P.sem(tag + n) for n in ("w", "m", "sq", "o", "rt", "rc", "ev", "st"))
    NB = 64
    NG = NB * 4

    def tail(n):
        blk, tt = divmod(n, 4)
        pm = C.PS[(n % 4) * 2]
        pq = C.PS[(n % 4) * 2 + 1]
        tok = slice(tt * 512, (tt + 1) * 512)
        col = (gq_col if blk < 32 else gk_col)
        P.wait("tensor", s_sq, n + 1)
        P.op("tensor", lambda e: e.matmul(pq[:], lhsT=C.ones128[:], rhs=sq[n % 2], start=True, stop=True),
             inc=s_o)
        P.wait("scalar", s_o, n + 1)
        P.op("scalar", lambda e: e.activation(out=rt[n % 2], in_=pq[:], func=AF.Sqrt, bias=C.epsb[:],
                                              scale=1.0), inc=s_rt)
        P.wait("vector", s_rt, n + 1)
        P.op("vector", lambda e: e.reciprocal(out=rt[n % 2], in_=rt[n % 2]), inc=s_rc)
        P.wait("vector", s_rc, n + 1)
        if tt == 0:
            P.wait("vector", s_st, 16 * (blk - 1))
        P.op("vector", lambda e: e.scalar_tensor_tensor(
            out=ob[blk % 2][:, tok], in0=pm[:], scalar=C.gains[:, col:col + 1], in1=rt[n % 2],
            op0=ALU.mult, op1=ALU.mult), inc=s_ev)
        if tt == 3:
            P.wait("sync", s_ev, 4 * (blk + 1))
            dst = qT[blk] if blk < 32 else kT[blk - 32]
            P.dma("sync", dst, ob[blk % 2], inc=s_st)

    for blk in range(NB):
        P.wait("gpsimd", s_m, 4 * (blk - 1))
        P.dma("gpsimd", wb[blk % 2], wqk[blk], inc=s_w)
        wv = wb[blk % 2].rearrange("p (c j) -> p c j", c=DC)
        for tt in range(4):
            n = blk * 4 + tt
            pm = C.PS[(n % 4) * 2]
            tok = slice(tt * 512, (tt + 1) * 512)
            if tt == 0:
                P.wait("tensor", s_w, 16 * (blk + 1))
            P.wait("tensor", s_ev, n - 3)
            for c in range(DC):
                fn = lambda e, c=c, pm=pm, wv=wv, tok=tok: e.matmul(
                    pm[:], lhsT=wv[:, c, :], rhs=hT[:, c, tok], start=(c == 0), stop=(c == DC - 1))
                P.op("tensor", fn, inc=(s_m if c == DC - 1 else None))
            P.wait("scalar", s_m, n + 1)
            P.wait("scalar", s_o, n - 1)
            P.op("scalar", lambda e, n=n, pm=pm: e.activation(out=sq[n % 2], in_=pm[:], func=AF.Square),
                 inc=s_sq)
            if n >= 1:
                tail(n - 1)
    tail(NG - 1)
    return (s_st, s_st.n)


def attn_iters():
    its = []
    for h in range(32):
        for j in range(4):
            for kb in range(8 * j + 7, -1, -1):
                its.append((h, j, kb))
    return its


def phase_attn(P, C, qT, kTg, vg, masks_sb, tag):
    oT = C.R[:, :].rearrange("p (c t) -> p c t", c=DC)
    kb_ = [C.WA[:, 0:4096].rearrange("p (r t) -> p r t", r=2),
           C.WA[:, 8192:12288].rearrange("p (r t) -> p r t", r=2)]
    vb_ = [C.WA[:, 4096:8192].rearrange("p (r i d) -> p r i d", r=2, i=NBLK),
           C.WA[:, 12288:16384].rearrange("p (r i d) -> p r i d", r=2, i=NBLK)]
    qb_ = [C.BA[:, 0:2048], C.BA[:, 2048:4096]]
    eb = [C.FA[:, 0:512], C.FA[:, 512:1024]]
    wk = C.FA[:, 1024:8192].bitcast(BF16)
    spb = [wk[:, k * 512:(k + 1) * 512] for k in range(3)]
    ab = [wk[:, 1536 + k * 512:1536 + (k + 1) * 512] for k in range(2)]
    acc = [wk[:, 2560 + k * 512:2560 + (k + 1) * 512] for k in range(2)]
    PZ = [C.PS[0], C.PS[1], C.PS[2]]
    PL = [C.PS[3], C.PS[4]]
    PO = [C.PS[5], C.PS[6]]
    s_hl, s_z, s_e, s_sp, s_la, s_acc, s_a, s_av, s_oe = (
        P.sem(tag + n) for n in ("hl", "z", "e", "sp", "la", "acc", "a", "av", "oe"))
    its = attn_iters()
    N = len(its)
    kqv = [kTg[h // 2, :, h % 2].rearrange("r p t -> p r t") for h in range(32)]
    vv = [vg[h // 2, :, h % 2].rearrange("r p i d -> p r i d") for h in range(32)]
    head_end = {}
    cnt = 0
    for h in range(32):
        cnt += 80
        head_end[h] = cnt

    def meta(i):
        h, j, kb = its[i]
        o = kb - 8 * j
        first = (o == 7)
        last = (kb == 0)
        r = (kb & 1) ^ ((kb >> 1) & 1)
        il = kb >> 1
        tidx = h * 4 + j
        return h, j, kb, o, first, last, r, il, tidx

    def load_head(h):
        if h >= 2:
            P.wait("sync", s_av, head_end[h - 2])
        P.dma("sync", kb_[h % 2], kqv[h], inc=s_hl)
        P.dma("sync", vb_[h % 2], vv[h], inc=s_hl)
        P.dma("sync", qb_[h % 2], qT[h], inc=s_hl)

    def Z(i):
        h, j, kb, o, first, last, r, il, tidx = meta(i)
        if i % 80 == 0:
            P.wait("tensor", s_hl, 48 * (h + 1))
        P.wait("tensor", s_e, i - 2)
        pz = PZ[i % 3]
        hm = (o >= 0)
        P.op("tensor", lambda e: e.matmul(pz[:], lhsT=kb_[h % 2][:, r, il * 128:(il + 1) * 128],
                                          rhs=qb_[h % 2][:, j * 512:(j + 1) * 512], start=True, stop=not hm),
             inc=(None if hm else s_z))
        if hm:
            P.op("tensor", lambda e: e.matmul(pz[:], lhsT=C.ident[:], rhs=masks_sb[:, o, :],
                                              start=False, stop=True), inc=s_z)

    def E(i):
        P.wait("scalar", s_z, i + 1)
        pz = PZ[i % 3]
        P.op("scalar", lambda e: e.activation(out=eb[i % 2], in_=pz[:], func=AF.Exp), inc=s_e)

    def SP(i):
        P.wait("scalar", s_e, i + 1)
        P.wait("scalar", s_la, i - 2)
        P.wait("scalar", s_acc, i - 2)
        P.op("scalar", lambda e: e.activation(out=spb[i % 3], in_=eb[i % 2], func=AF.Ln, bias=1.0, scale=1.0),
             inc=s_sp)

    def LA(i):
        h, j, kb, o, first, last, r, il, tidx = meta(i)
        P.wait("tensor", s_sp, i + 1)
        if not first:
            P.wait("tensor", s_acc, i)
        P.wait("tensor", s_a, i - 1)
        pl = PL[i % 2]
        hm = (o >= 0)
        P.op("tensor", lambda e: e.matmul(pl[:], lhsT=kb_[h % 2][:, r, il * 128:(il + 1) * 128],
                                          rhs=qb_[h % 2][:, j * 512:(j + 1) * 512], start=True, stop=False))
        if hm:
            P.op("tensor", lambda e: e.matmul(pl[:], lhsT=C.ident[:], rhs=masks_sb[:, o, :],
                                              start=False, stop=False))
        P.op("tensor", lambda e: e.matmul(pl[:], lhsT=C.negtri[:], rhs=spb[i % 3], start=False, stop=first),
             inc=(s_la if first else None))
        if not first:
            P.op("tensor", lambda e: e.matmul(pl[:], lhsT=C.negones[:], rhs=acc[tidx % 2], start=False,
                                              stop=True), inc=s_la)

    def ACC(i):
        h, j, kb, o, first, last, r, il, tidx = meta(i)
        P.wait("vector", s_sp, i + 1)
        P.wait("vector", s_la, i + 1)
        if first:
            P.op("vector", lambda e: e.tensor_copy(out=acc[tidx % 2], in_=spb[i % 3]), inc=s_acc)
        else:
            P.wait("vector", s_acc, i)
            P.op("vector", lambda e: e.tensor_tensor(out=acc[tidx % 2], in0=acc[tidx % 2], in1=spb[i % 3],
                                                     op=ALU.add), inc=s_acc)

    def A(i):
        P.wait("scalar", s_la, i + 1)
        P.wait("scalar", s_av, i - 1)
        pl = PL[i % 2]
        P.op("scalar", lambda e: e.activation(out=ab[i % 2], in_=pl[:], func=AF.Exp), inc=s_a)

    def AV(i):
        h, j, kb, o, first, last, r, il, tidx = meta(i)
        P.wait("tensor", s_a, i + 1)
        if first:
            P.wait("tensor", s_oe, tidx - 1)
        po = PO[tidx % 2]
        P.op("tensor", lambda e: e.matmul(po[:], lhsT=vb_[h % 2][:, r, il, :], rhs=ab[i % 2],
                                          start=first, stop=last), inc=s_av)
        if last:
            P.wait("vector", s_av, i + 1)
            P.op("vector", lambda e: e.tensor_copy(out=oT[:, h, j * 512:(j + 1) * 512], in_=po[:]), inc=s_oe)

    load_head(0)
    load_head(1)
    for s in range(N + 2):
        if s < N:
            h = its[s][0]
            if s % 80 == 0 and h >= 1 and h + 1 < 32:
                load_head(h + 1)
            Z(s)
            E(s)
            SP(s)
        if 1 <= s <= N:
            LA(s - 1)
            A(s - 1)
            ACC(s - 1)
        if 2 <= s <= N + 1:
            AV(s - 2)
    return (s_oe, 128)


def ffn_layer(P, C, gcol, wgu, wd, actT, xsrc, xdst, tag):
    d = phase_norm(P, C, xsrc, gcol, tag + "n")
    barrier(P, [d])
    d = phase_ffn_g(P, C, wgu, actT, tag + "g")
    barrier(P, [d])
    d = phase_proj_res(P, C, wd, FH, 2, 2, actT, xsrc, xdst, tag + "d")
    barrier(P, [d])
    return d


def gmlp_layer(P, C, j, W, S, xsrc, xdst, tag):
    d = phase_norm(P, C, xsrc, GCOL["a_norm"][j], tag + "n")
    barrier(P, [d])
    d = phase_tm(P, C, W[f"a_wv{j}"], S["vtm"], "gelu", C.ssqp, tag + "v")
    barrier(P, [d])
    d = phase_gmlp_gate(P, C, W[f"a_wu{j}"], W[f"a_ws{j}"], W[f"a_bs{j}"], S["vtm"], S["gT"],
                        GCOL["a_vn"][j], C.ssqp, tag + "g")
    barrier(P, [d])
    d = phase_proj_res(P, C, W[f"a_wo{j}"], DC, 1, 1, S["gT"], xsrc, xdst, tag + "o")
    barrier(P, [d])
    return d


def bproj_layer(P, C, j, W, S, xsrc, qT, kT, vl, tag):
    d = phase_norm(P, C, xsrc, GCOL["b_norm"][j], tag + "n")
    barrier(P, [d])
    d = phase_qk(P, C, W[f"b_wqk{j}"], qT, kT, GCOL["q"][j], GCOL["k"][j], tag + "q")
    barrier(P, [d])
    d = phase_tm(P, C, W[f"b_wv{j}"], vl, "copy", None, tag + "v")
    barrier(P, [d])
    return d


def battn_layer(P, C, j, W, S, masks, qT, kTg, vg, xsrc, xdst, tag):
    msb = C.FA[:, 4096:6144].bitcast(BF16).rearrange("p (o t) -> p o t", o=8)
    sm = P.sem(tag + "mk")
    P.dma("gpsimd", C.FA[:, 4096:6144].bitcast(BF16), masks, inc=sm)
    barrier(P, [(sm, 16)])
    d = phase_attn(P, C, qT, kTg, vg, msb, tag + "a")
    barrier(P, [d])
    d = phase_proj_res(P, C, W[f"b_wo{j}"], DC, 1, 1, None, xsrc, xdst, tag + "o")
    barrier(P, [d])
    return d


WSHAPES = {}
for _j in range(2):
    WSHAPES[f"a_wv{_j}"] = [16, 128, 8192]
    WSHAPES[f"a_wu{_j}"] = [32, 128, 4096]
    WSHAPES[f"a_ws{_j}"] = [128, 4096]
    WSHAPES[f"a_bs{_j}"] = [32, 128]
    WSHAPES[f"a_wo{_j}"] = [1, 32, 128, 4096]
    WSHAPES[f"b_wqk{_j}"] = [64, 128, 4096]
    WSHAPES[f"b_wv{_j}"] = [16, 128, 8192]
    WSHAPES[f"b_wo{_j}"] = [1, 32, 128, 4096]
for _l in range(4):
    WSHAPES[f"wgu{_l}"] = [FC, 128, 8192]
    WSHAPES[f"wd{_l}"] = [2, DC, 128, FH * 128]

SEG_STEPS = {
    0: ["A0", "F0", "P0"],
    1: ["T0", "F1", "A1", "F2", "P1"],
    2: ["T1", "F3"],
    "t": ["T0"],
    "all": ["A0", "F0", "P0", "G0", "T0", "F1", "A1", "F2", "P1", "G1", "T1", "F3"],
}
STEP_W = {"A0": ["a_wv0", "a_wu0", "a_ws0", "a_bs0", "a_wo0"], "A1": ["a_wv1", "a_wu1", "a_ws1", "a_bs1", "a_wo1"],
          "P0": ["b_wqk0", "b_wv0"], "P1": ["b_wqk1", "b_wv1"], "T0": ["b_wo0"], "T1": ["b_wo1"],
          "F0": ["wgu0", "wd0"], "F1": ["wgu1", "wd1"], "F2": ["wgu2", "wd2"], "F3": ["wgu3", "wd3"],
          "G0": [], "G1": []}
PAIRS = [[0, 1], [2, 3], [4, 5], [6, 7]]


def seg_weight_names(seg):
    out = []
    for st in SEG_STEPS[seg]:
        out += STEP_W[st]
    return out


def build_program(seg):
    nc = bass.Bass("TRN2", target_bir_lowering=False)
    steps = SEG_STEPS[seg]
    fused = (seg == "all")
    W = {}
    for name in seg_weight_names(seg):
        W[name] = nc.dram_tensor(name, WSHAPES[name], F32, kind="ExternalInput").ap()
    gains = nc.dram_tensor("gains", [128, NGAIN], F32, kind="ExternalInput").ap()
    consts = nc.dram_tensor("consts", [128, 3, 128], F32, kind="ExternalInput").ap()
    masks = nc.dram_tensor("masks", [128, 4096], F32, kind="ExternalInput").ap()
    xin = nc.dram_tensor("xin", [DC, 128, T], F32, kind="ExternalInput").ap()
    xo = nc.dram_tensor("xo", [DC, 128, T], F32, kind="ExternalOutput").ap()
    S = {}
    S["actT"] = nc.dram_tensor("actT", [FC, 128, T], BF16, kind="Internal").ap()
    S["vtm"] = nc.dram_tensor("vtm", [32, 128, NBLK, 128], BF16, kind="Internal").ap()
    S["gT"] = nc.dram_tensor("gT", [32, 128, T], BF16, kind="Internal").ap()
    if fused:
        qT = nc.dram_tensor("qT", [32, 128, T], BF16, kind="Internal").ap()
        kT = nc.dram_tensor("kT", [32, 128, T], BF16, kind="Internal").ap()
        vl = nc.dram_tensor("vl", [32, 128, NBLK, 128], BF16, kind="Internal").ap()
        kTgs = [nc.dram_tensor(f"kTg{i}", [16, 2, 2, 128, T], BF16, kind="Internal").ap() for i in range(2)]
        vgs = [nc.dram_tensor(f"vg{i}", [16, 2, 2, 128, NBLK, 128], BF16, kind="Internal").ap() for i in range(2)]
        q_in = qT
    else:
        if seg in (0, 1):
            qT = nc.dram_tensor("qTo", [32, 128, T], BF16, kind="ExternalOutput").ap()
            kT = nc.dram_tensor("kTo", [32, 128, T], BF16, kind="ExternalOutput").ap()
            vl = nc.dram_tensor("vlo", [32, 128, NBLK, 128], BF16, kind="ExternalOutput").ap()
        if seg in (1, 2, "t"):
            q_in = nc.dram_tensor("qin", [32, 128, T], BF16, kind="ExternalInput").ap()
            kTg = nc.dram_tensor("kTg", [16, 2, 2, 128, T], BF16, kind="ExternalInput").ap()
            vg = nc.dram_tensor("vg", [16, 2, 2, 128, NBLK, 128], BF16, kind="ExternalInput").ap()
    with ExitStack() as st:
        P = Prog(nc, st)
        C = setup_ctx(P, gains[:, :], NGAIN, consts[:, :, :])
        xcur = xin
        d = None
        for step in steps:
            kind, idx = step[0], int(step[1])
            tag = step.lower()
            if kind == "A":
                d = gmlp_layer(P, C, idx, W, S, xcur, xo, tag)
                xcur = xo
            elif kind == "F":
                d = ffn_layer(P, C, GCOL["f_norm"][idx], W[f"wgu{idx}"], W[f"wd{idx}"], S["actT"], xcur, xo, tag)
                xcur = xo
            elif kind == "P":
                d = bproj_layer(P, C, idx, W, S, xcur, qT, kT, vl, tag)
            elif kind == "G":
                kTg, vg = kTgs[idx], vgs[idx]
                sg = P.sem(tag + "cc")
                k2 = kT.rearrange("h p t -> (h p) t")
                v2 = vl.rearrange("h p i d -> (h p) (i d)")
                pairs = []
                for ck in range(16):
                    pairs.append((k2[ck * 256:(ck + 1) * 256, :], kTg[ck].rearrange("r h p t -> (r h p) t")))
                    pairs.append((v2[ck * 256:(ck + 1) * 256, :], vg[ck].rearrange("r h p i d -> (r h p) (i d)")))
                for (src, dst) in pairs:
                    def fn(e, src=src, dst=dst):
                        return e.collective_compute("AllGather", ALU.bypass, replica_groups=PAIRS,
                                                    ins=[src.opt()], outs=[dst.opt()])
                    P.op("gpsimd", fn, inc=sg)
                d = (sg, sg.n)
                barrier(P, [d])
            elif kind == "T":
                if fused:
                    kTg, vg = kTgs[idx], vgs[idx]
                d = battn_layer(P, C, idx, W, S, masks, q_in, kTg, vg, xcur, xo, tag)
                xcur = xo
        barrier(P, [d])
        P.emit()
    return nc


def host_layout_ffn(w_gate, w_up, w_down):
    g = w_gate.reshape(DC, 128, FC, 128).transpose(2, 1, 0, 3)
    u = w_up.reshape(DC, 128, FC, 128).transpose(2, 1, 0, 3)
    wgu = np.empty((FC, 128, 2, DC, 128), np.float32)
    wgu[:, :, 0] = g
    wgu[:, :, 1] = u
    wgu = wgu.reshape(FC, 128, 2 * DC * 128)
    wd = np.ascontiguousarray(
        w_down.reshape(2, FH, 128, DC, 128).transpose(0, 3, 2, 1, 4)).reshape(2, DC, 128, FH * 128)
    return wgu, wd


def gain_cols(g):
    return np.ascontiguousarray(g.reshape(DC, 128).T)


def block_perm(r):
    return [2 * i + (r ^ (i & 1)) for i in range(NBLK)]


def x_to_core(x):
    outs = []
    for c in range(NCORES):
        b, r = c // 2, c % 2
        xb = x[b].reshape(SEQ // 128, 128, D)[block_perm(r)].reshape(T, D)
        outs.append(np.ascontiguousarray(xb.T).reshape(DC, 128, T))
    return outs


def core_to_x(outs):
    y = np.empty((BATCH, SEQ, D), np.float32)
    for c in range(NCORES):
        b, r = c // 2, c % 2
        xt = outs[c].reshape(D, T).T.reshape(NBLK, 128, D)
        yb = y[b].reshape(SEQ // 128, 128, D)
        yb[block_perm(r)] = xt
    return y


def tile_cols(w, nblk, width):
    return np.ascontiguousarray(
        w.reshape(DC, 128, nblk, width).transpose(2, 1, 0, 3)).reshape(nblk, 128, DC * width)


def host_weights(inp, names):
    Wd = {}
    for name in names:
        base, j = name[:-1], int(name[-1])
        if base == "a_wv":
            Wd[name] = tile_cols(inp["a_w_in"][j][:, 4096:], 16, 256)
        elif base == "a_wu":
            Wd[name] = tile_cols(inp["a_w_in"][j][:, :4096], 32, 128)
        elif base == "a_ws":
            Wd[name] = np.ascontiguousarray(inp["a_w_s"][j].transpose(2, 0, 1)).reshape(128, 4096)
        elif base == "a_bs":
            Wd[name] = np.ascontiguousarray(inp["a_b_s"][j])
        elif base == "a_wo":
            Wd[name] = tile_cols(inp["a_w_out"][j], 32, 128).reshape(1, 32, 128, 4096)
        elif base == "b_wqk":
            Wd[name] = tile_cols(inp["b_w_qkv"][j][:, :8192], 64, 128)
        elif base == "b_wv":
            Wd[name] = tile_cols(inp["b_w_qkv"][j][:, 8192:], 16, 256)
        elif base == "b_wo":
            Wd[name] = tile_cols(inp["b_w_out"][j], 32, 128).reshape(1, 32, 128, 4096)
        elif base == "wgu":
            Wd[name], Wd[f"wd{j}"] = host_layout_ffn(inp["f_w_gate"][j], inp["f_w_up"][j], inp["f_w_down"][j])
        elif base == "wd":
            pass
        else:
            raise KeyError(name)
    return Wd


def host_gains(inp):
    g = np.zeros((128, NGAIN), np.float32)
    for j in range(2):
        g[:, GCOL["a_norm"][j]:GCOL["a_norm"][j] + 32] = gain_cols(inp["a_norm"][j])
        g[:, GCOL["a_vn"][j]:GCOL["a_vn"][j] + 32] = gain_cols(inp["a_v_norm"][j])
        g[:, GCOL["b_norm"][j]:GCOL["b_norm"][j] + 32] = gain_cols(inp["b_norm"][j])
        g[:, GCOL["q"][j]] = inp["b_q_norm"][j]
        g[:, GCOL["k"][j]] = inp["b_k_norm"][j]
    for i in range(4):
        g[:, GCOL["f_norm"][i]:GCOL["f_norm"][i] + 32] = gain_cols(inp["f_norm"][i])
    return g


def host_consts():
    c = np.zeros((128, 3, 128), np.float32)
    c[:, 0, :] = np.eye(128, dtype=np.float32)
    jj, ss = np.meshgrid(np.arange(128), np.arange(128), indexing="ij")
    c[:, 1, :] = np.where(jj >= ss, -1.0, 0.0)
    c[:, 2, :] = -1.0
    return c


def host_masks(r):
    rel = [i * 2 + (r ^ (i & 1)) for i in range(4)]
    m = np.zeros((128, 8, 4, 128), np.float32)
    ss, tt = np.meshgrid(np.arange(128), np.arange(128), indexing="ij")
    diag = np.where(ss >= tt, NEG, 0.0).astype(np.float32)
    for o in range(8):
        for i in range(4):
            if rel[i] == o:
                m[:, o, i, :] = diag
            elif rel[i] < o:
                m[:, o, i, :] = NEG
    return m.reshape(128, 4096)


_PROG_CACHE = {}


def _get_prog(seg):
    if seg not in _PROG_CACHE:
        _PROG_CACHE[seg] = build_program(seg)
    return _PROG_CACHE[seg]


FUSED = True


def _pair_gather(arrs):
    out = []
    for c in range(NCORES):
        p = (c // 2) * 2
        a = arrs[p].reshape((16, 2) + arrs[p].shape[1:])
        b = arrs[p + 1].reshape((16, 2) + arrs[p + 1].shape[1:])
        out.append(np.stack([a, b], axis=1))
    return out


def kernel(**inputs):
    inp = {k: np.asarray(v) for k, v in inputs.items()}
    xs = x_to_core(inp["x"].astype(np.float32, copy=False))
    gains = host_gains(inp)
    consts = host_consts()
    masks = [host_masks(c % 2) for c in range(NCORES)]
    cores = list(range(NCORES))
    if FUSED:
        Wd = host_weights(inp, seg_weight_names("all"))
        ims = []
        for c in cores:
            im = {"xin": xs[c], "gains": gains, "consts": consts, "masks": masks[c]}
            im.update(Wd)
            ims.append(im)
        res = run_bass_kernel_spmd(_get_prog("all"), ims, core_ids=cores)
        return core_to_x([res.results[c]["xo"] for c in cores])
    xcur = xs
    q = kg = vg = None
    for seg in (0, 1, 2):
        Wd = host_weights(inp, seg_weight_names(seg))
        ims = []
        for c in cores:
            im = {"xin": xcur[c], "gains": gains, "consts": consts, "masks": masks[c]}
            if seg >= 1:
                im.update({"qin": q[c], "kTg": kg[c], "vg": vg[c]})
            im.update(Wd)
            ims.append(im)
        res = run_bass_kernel_spmd(_get_prog(seg), ims, core_ids=cores)
        del ims, Wd
        xcur = [res.results[c]["xo"] for c in cores]
        if seg < 2:
            q = [res.results[c]["qTo"] for c in cores]
            kg = _pair_gather([res.results[c]["kTo"] for c in cores])
            vg = _pair_gather([res.results[c]["vlo"] for c in cores])
    return core_to_x(xcur)
```

```python
import math
from contextlib import ExitStack

import numpy as np
import ml_dtypes

import concourse.bass as bass
import concourse.mybir as mybir
from concourse.bass_utils import run_bass_kernel_spmd

F32 = mybir.dt.float32
BF16 = mybir.dt.bfloat16
AF = mybir.ActivationFunctionType
ALU = mybir.AluOpType

D = 4096
DC = 32
SEQ = 4096
BATCH = 4
T = 2048
NBLK = 16
F = 11008
FC = 86
FH = 43
DEPTH = 4
EPS = 1e-6
NCORES = 8
NEG = -30000.0


class PhysSem:
    def __init__(self, h):
        self.h = h
        self.count = 0


class Sem:
    def __init__(self, phys, gen):
        self.phys = phys
        self.h = phys.h
        self.base = phys.count
        self.n = 0
        self.gen = gen


class Prog:
    def __init__(self, nc, stack):
        self.nc = nc
        self.stack = stack
        self.q = {e: [] for e in ("sync", "scalar", "vector", "gpsimd", "tensor")}
        self.nsem = 0
        self.gen = 0
        self.free = []
        self.live = []

    def sem(self, name):
        if self.free:
            phys = self.free.pop()
        else:
            self.nsem += 1
            phys = PhysSem(self.stack.enter_context(self.nc.semaphore(f"sem_{self.nsem}")))
        v = Sem(phys, self.gen)
        self.live.append(v)
        return v

    def next_gen(self):
        self.gen += 1
        keep = []
        for v in self.live:
            if v.gen <= self.gen - 2:
                v.phys.count = v.base + v.n
                self.free.append(v.phys)
            else:
                keep.append(v)
        self.live = keep

    def sb(self, name, shape, dt):
        return self.stack.enter_context(self.nc.sbuf_tensor(name, shape, dt))

    def psum(self, name, shape, dt):
        return self.stack.enter_context(self.nc.psum_tensor(name, shape, dt))

    def op(self, eng, fn, inc=None):
        if inc is None:
            self.q[eng].append(fn)
            return None
        step = 16 if getattr(fn, "_is_dma", False) else 1
        inc.n += step
        h = inc.h

        def f(e, fn=fn, h=h, step=step):
            fn(e).then_inc(h, step)
        self.q[eng].append(f)
        return inc.n

    def dma(self, eng, out, in_, inc=None):
        def fn(e, out=out, in_=in_):
            return e.dma_start(out=out, in_=in_)
        fn._is_dma = True
        return self.op(eng, fn, inc)

    def wait(self, eng, sem, val):
        if val is None or val <= 0:
            return
        h = sem.h
        val = val + sem.base
        self.q[eng].append(lambda e, h=h, val=val: e.wait_ge(h, val))

    def emit(self):
        nc = self.nc
        with nc.Block() as block:
            @block.sync
            def _(e):
                for f in self.q["sync"]:
                    f(e)

            @block.scalar
            def _(e):
                for f in self.q["scalar"]:
                    f(e)

            @block.vector
            def _(e):
                for f in self.q["vector"]:
                    f(e)

            @block.gpsimd
            def _(e):
                for f in self.q["gpsimd"]:
                    f(e)

            @block.tensor
            def _(e):
                for f in self.q["tensor"]:
                    f(e)


class Ctx:
    pass


GCOL = {"a_norm": (0, 32), "a_vn": (64, 96), "b_norm": (128, 160), "f_norm": (192, 224, 256, 288),
        "q": (320, 321), "k": (322, 323)}
NGAIN = 324


def setup_ctx(P, gains_ap, ngain, consts_ap=None):
    nc = P.nc
    C = Ctx()
    C.R = P.sb("R", [128, 65536], BF16)
    C.WA = P.sb("WA", [128, 16384], BF16)
    C.FA = P.sb("FA", [128, 8192], F32)
    C.BA = P.sb("BA", [128, 4096], BF16)
    C.gains = P.sb("gains_sb", [128, ngain], F32)
    C.ones = P.sb("ones", [128, 128], BF16)
    C.ones128 = P.sb("ones128", [128, 128], BF16)
    C.rs = P.sb("rs", [128, 2, 128], F32)
    C.epsb = P.sb("epsb", [128, 1], F32)
    C.PS = [P.psum(f"ps{i}", [128, 512], F32) for i in range(8)]
    C.init = P.sem("init")
    P.op("vector", lambda e: e.memset(C.ones[:], 1.0 / D))
    P.op("vector", lambda e: e.memset(C.ones128[:], 1.0 / 128))
    P.op("vector", lambda e: e.memset(C.epsb[:], EPS), inc=C.init)
    v = P.dma("sync", C.gains[:], gains_ap, inc=C.init)
    if consts_ap is not None:
        C.cst = P.sb("cst", [128, 3, 128], BF16)
        C.ident = C.cst[:, 0, :]
        C.negtri = C.cst[:, 1, :]
        C.negones = C.cst[:, 2, :]
        C.ssqp = P.sb("ssqp", [128, 16, 16], F32)
        P.dma("gpsimd", C.cst[:], consts_ap, inc=C.init)
        P.wait("vector", C.init, C.init.n)
        for col in GCOL["q"]:
            P.op("vector", lambda e, col=col: e.tensor_scalar(
                out=C.gains[:, col:col + 1], in0=C.gains[:, col:col + 1], scalar1=1.0 / math.sqrt(128.0),
                scalar2=None, op0=ALU.mult), inc=C.init)
    C.init_val = C.init.n
    for eng in ("scalar", "vector", "tensor", "gpsimd"):
        P.wait(eng, C.init, C.init_val)
    return C


def barrier(P, dones):
    for eng in ("sync", "scalar", "vector", "gpsimd", "tensor"):
        for (s, v) in dones:
            P.wait(eng, s, v)
    P.next_gen()


def phase_norm(P, C, xsrc, gcol, tag):
    hT = C.R[:, :].rearrange("p (c t) -> p c t", c=DC)
    xv = xsrc.rearrange("c p t -> p c t")
    xt = [C.FA[:, 0:4096].rearrange("p (c t) -> p c t", c=DC),
          C.FA[:, 4096:8192].rearrange("p (c t) -> p c t", c=DC)]
    sq = C.BA[:, 0:4096].rearrange("p (c t) -> p c t", c=DC)
    s_ld, s_sq, s_ss, s_rt, s_dv = (P.sem(tag + n) for n in ("ld", "sq", "ss", "rt", "dv"))
    g_b = C.gains[:, gcol:gcol + DC].unsqueeze(2).broadcast_to([128, DC, 128])
    for i in range(NBLK):
        b = i % 2
        tok = slice(i * 128, (i + 1) * 128)
        P.wait("sync", s_dv, 3 * (i - 1))
        P.dma("sync", xt[b], xv[:, :, tok], inc=s_ld)
        P.wait("scalar", s_ld, 16 * (i + 1))
        P.wait("scalar", s_ss, i)
        P.op("scalar", lambda e, b=b: e.activation(out=sq, in_=xt[b], func=AF.Square), inc=s_sq)
        P.wait("tensor", s_sq, i + 1)
        P.wait("tensor", s_rt, i - 1)
        ps = C.PS[b][:, 0:128]
        for c in range(DC):
            fn = lambda e, c=c, ps=ps: e.matmul(ps, lhsT=C.ones[:], rhs=sq[:, c, :],
                                                start=(c == 0), stop=(c == DC - 1))
            P.op("tensor", fn, inc=(s_ss if c == DC - 1 else None))
        P.wait("scalar", s_ss, i + 1)
        P.wait("scalar", s_dv, 3 * (i - 1))
        P.op("scalar", lambda e, b=b, ps=ps: e.activation(out=C.rs[:, b, :], in_=ps, func=AF.Sqrt,
                                                          bias=C.epsb[:], scale=1.0), inc=s_rt)
        P.wait("vector", s_rt, i + 1)
        P.op("vector", lambda e, b=b: e.reciprocal(out=C.rs[:, b, :], in_=C.rs[:, b, :]), inc=s_dv)
        P.wait("vector", s_dv, 3 * i + 1)
        rb = C.rs[:, b, :].unsqueeze(1).broadcast_to([128, DC, 128])
        P.op("vector", lambda e, b=b, rb=rb: e.tensor_tensor(out=xt[b], in0=xt[b], in1=rb, op=ALU.mult),
             inc=s_dv)
        P.wait("vector", s_dv, 3 * i + 2)
        P.op("vector", lambda e, b=b, tok=tok: e.tensor_tensor(out=hT[:, :, tok], in0=xt[b], in1=g_b,
                                                               op=ALU.mult), inc=s_dv)
    return (s_dv, 3 * NBLK)


def phase_ffn_g(P, C, wgu, actT, tag):
    hT = C.R[:, :].rearrange("p (c t) -> p c t", c=DC)
    wb = [C.WA[:, 0:8192], C.WA[:, 8192:16384]]
    sg = [C.FA[:, 0:512], C.FA[:, 512:1024]]
    ab = [C.BA[:, 0:2048], C.BA[:, 2048:4096]]
    s_w, s_g, s_u, s_a, s_d, s_st = (P.sem(tag + n) for n in ("w", "g", "u", "a", "d", "st"))
    for fc in range(FC):
        wbuf = wb[fc % 2]
        P.wait("gpsimd", s_u, 4 * (fc - 1))
        P.dma("gpsimd", wbuf, wgu[fc], inc=s_w)
        wv = wbuf.rearrange("p (s c j) -> p s c j", s=2, c=DC)
        for tt in range(4):
            gi = fc * 4 + tt
            pg = C.PS[(gi % 4) * 2]
            pu = C.PS[(gi % 4) * 2 + 1]
            tok = slice(tt * 512, (tt + 1) * 512)
            if tt == 0:
                P.wait("tensor", s_w, 16 * (fc + 1))
            P.wait("tensor", s_d, gi - 3)
            for s, pp, sem in ((0, pg, s_g), (1, pu, s_u)):
                for c in range(DC):
                    fn = lambda e, s=s, c=c, pp=pp, wv=wv, tok=tok: e.matmul(
                        pp[:], lhsT=wv[:, s, c, :], rhs=hT[:, c, tok], start=(c == 0), stop=(c == DC - 1))
                    P.op("tensor", fn, inc=(sem if c == DC - 1 else None))
            P.wait("scalar", s_g, gi + 1)
            P.wait("scalar", s_d, gi - 1)
            P.op("scalar", lambda e, gi=gi, pg=pg: e.activation(out=sg[gi % 2], in_=pg[:], func=AF.Silu),
                 inc=s_a)
            P.wait("vector", s_a, gi + 1)
            P.wait("vector", s_u, gi + 1)
            if tt == 0:
                P.wait("vector", s_st, 16 * (fc - 1))
            P.op("vector", lambda e, gi=gi, pu=pu, fc=fc, tok=tok: e.tensor_tensor(
                out=ab[fc % 2][:, tok], in0=sg[gi % 2], in1=pu[:], op=ALU.mult), inc=s_d)
        P.wait("sync", s_d, 4 * (fc + 1))
        P.dma("sync", actT[fc], ab[fc % 2], inc=s_st)
    return (s_st, 16 * FC)


def phase_proj_res(P, C, wd, KC, NTH, NFH, actT, xsrc, xdst, tag):
    TH = T // NTH
    NT2 = TH // 512
    Rv = C.R[:, 0:KC * TH].rearrange("p (c t) -> p c t", c=KC)
    wb = [C.WA[:, 0:KC * 128], C.WA[:, 8192:8192 + KC * 128]]
    xs = [C.FA[:, k * 512:(k + 1) * 512] for k in range(4)]
    s_al, s_w, s_m, s_xl, s_dv, s_st = (P.sem(tag + n) for n in ("al", "w", "m", "xl", "dv", "st"))
    groups = []
    for th in range(NTH):
        for fh in range(NFH):
            for dc in range(DC):
                for t2 in range(NT2):
                    groups.append((th, fh, dc, t2))
    NG = len(groups)
    store_val = {}

    def issue_xload(n):
        th, fh, dc, t2 = groups[n]
        tok = slice(th * TH + t2 * 512, th * TH + (t2 + 1) * 512)
        P.wait("sync", s_st, 16 * (n - 3))
        if fh >= 1:
            P.wait("sync", s_st, store_val[(th, fh - 1, dc, t2)])
        src = xsrc if fh == 0 else xdst
        P.dma("sync", xs[n % 4], src[dc, :, tok], inc=s_xl)

    n = 0
    nw = 0
    NP = NTH * NFH
    s_kq = P.sem(tag + "kq")
    passes = [(th, fh) for th in range(NTH) for fh in range(NFH)]
    NSPL = 4
    qr = [((KC * k) // NSPL, (KC * (k + 1)) // NSPL) for k in range(NSPL)]

    def load_act(p, q):
        th, fh = passes[p]
        av = actT.rearrange("c p t -> p c t")
        c0, c1 = qr[q]
        P.dma("sync", Rv[:, c0:c1, :], av[:, fh * KC + c0:fh * KC + c1, th * TH:(th + 1) * TH], inc=s_al)

    for p, (th, fh) in enumerate(passes):
        al_val = 0
        if actT is not None:
            if p == 0:
                for q in range(NSPL):
                    load_act(0, q)
            al_val = 16 * NSPL * (p + 1)
        if p == 0:
            for k in range(3):
                issue_xload(k)
        for dc in range(DC):
            wbuf = wb[nw % 2]
            P.wait("gpsimd", s_m, NT2 * (nw - 1))
            P.dma("gpsimd", wbuf, wd[fh, dc], inc=s_w)
            wv = wbuf.rearrange("p (c j) -> p c j", c=KC)
            inter = (actT is not None and dc == DC - 1 and p < NP - 1)
            P.wait("tensor", s_w, 16 * (nw + 1))
            if dc == 0:
                P.wait("tensor", s_al, al_val)
            for t2 in range(NT2):
                P.wait("tensor", s_dv, n + t2 - 7)
            if not inter:
                for t2 in range(NT2):
                    ps = C.PS[(n + t2) % 8]
                    tok = slice(t2 * 512, (t2 + 1) * 512)
                    for k in range(KC):
                        fn = lambda e, k=k, ps=ps, wv=wv, tok=tok: e.matmul(
                            ps[:], lhsT=wv[:, k, :], rhs=Rv[:, k, tok], start=(k == 0), stop=(k == KC - 1))
                        P.op("tensor", fn, inc=(s_m if k == KC - 1 else None))
            else:
                marks = {qr[q][1] - 1: q for q in range(NSPL - 1)}
                for k in range(KC):
                    for t2 in range(NT2):
                        ps = C.PS[(n + t2) % 8]
                        tok = slice(t2 * 512, (t2 + 1) * 512)
                        fn = lambda e, k=k, ps=ps, wv=wv, tok=tok: e.matmul(
                            ps[:], lhsT=wv[:, k, :], rhs=Rv[:, k, tok], start=(k == 0), stop=(k == KC - 1))
                        if k == KC - 1:
                            inc = s_m
                        elif t2 == NT2 - 1 and k in marks:
                            inc = s_kq
                        else:
                            inc = None
                        P.op("tensor", fn, inc=inc)
                for q in range(NSPL - 1):
                    P.wait("sync", s_kq, (NSPL - 1) * p + q + 1)
                    load_act(p + 1, q)
            for t2 in range(NT2):
                ps = C.PS[n % 8]
                P.wait("vector", s_m, n + 1)
                P.wait("vector", s_xl, 16 * (n + 1))
                P.op("vector", lambda e, n=n, ps=ps: e.tensor_tensor(
                    out=xs[n % 4], in0=xs[n % 4], in1=ps[:], op=ALU.add), inc=s_dv)
                if n + 3 < NG:
                    issue_xload(n + 3)
                if inter and t2 == NT2 - 1:
                    P.wait("sync", s_m, n + 1)
                    load_act(p + 1, NSPL - 1)
                P.wait("sync", s_dv, n + 1)
                tokg = slice(th * TH + t2 * 512, th * TH + (t2 + 1) * 512)
                store_val[(th, fh, dc, t2)] = P.dma("sync", xdst[dc, :, tokg], xs[n % 4], inc=s_st)
                n += 1
            nw += 1
    return (s_st, s_st.n)


def phase_tm(P, C, w, vout, mode, ssqp, tag):
    hT = C.R[:, :].rearrange("p (c t) -> p c t", c=DC)
    wb = [C.WA[:, 0:8192], C.WA[:, 8192:16384]]
    vb = [C.BA[:, k * 256:(k + 1) * 256] for k in range(4)]
    junk = C.FA[:, 0:256]
    vov = vout.rearrange("g p t c -> p g t c")
    s_w, s_m, s_ev, s_q, s_st = (P.sem(tag + n) for n in ("w", "m", "ev", "q", "st"))
    for fb in range(16):
        wbuf = wb[fb % 2]
        P.wait("gpsimd", s_m, 16 * (fb - 1))
        P.dma("gpsimd", wbuf, w[fb], inc=s_w)
        wv = wbuf.rearrange("p (c j) -> p c j", c=DC)
        for tb in range(NBLK):
            n = fb * 16 + tb
            ps = C.PS[n % 8][:, 0:256]
            if tb == 0:
                P.wait("tensor", s_w, 16 * (fb + 1))
            P.wait("tensor", s_ev, n - 7)
            for c in range(DC):
                fn = lambda e, c=c, ps=ps, wv=wv, tb=tb: e.matmul(
                    ps, lhsT=hT[:, c, tb * 128:(tb + 1) * 128], rhs=wv[:, c, :],
                    start=(c == 0), stop=(c == DC - 1))
                P.op("tensor", fn, inc=(s_m if c == DC - 1 else None))
            P.wait("scalar", s_m, n + 1)
            P.wait("scalar", s_st, 16 * (n - 3))
            if mode == "gelu":
                P.op("scalar", lambda e, n=n, ps=ps: e.activation(out=vb[n % 4], in_=ps, func=AF.Gelu),
                     inc=s_ev)
                P.wait("scalar", s_ev, n + 1)
                P.op("scalar", lambda e, n=n, tb=tb, fb=fb: e.activation(
                    out=junk, in_=vb[n % 4], func=AF.Square, accum_out=ssqp[:, tb, fb:fb + 1]), inc=s_q)
                P.wait("sync", s_q, n + 1)
            else:
                P.op("scalar", lambda e, n=n, ps=ps: e.activation(out=vb[n % 4], in_=ps, func=AF.Copy),
                     inc=s_ev)
                P.wait("sync", s_ev, n + 1)
            P.dma("sync", vov[:, 2 * fb:2 * fb + 2, tb, :],
                  vb[n % 4].rearrange("p (g c) -> p g c", g=2), inc=s_st)
    return (s_st, s_st.n)


def phase_gmlp_gate(P, C, wu, wsT, bs, vtm, gT, gvcol, ssqp, tag):
    hT = C.R[:, :].rearrange("p (c t) -> p c t", c=DC)
    ub = [C.WA[:, 0:4096], C.WA[:, 4096:8192]]
    vn = [C.WA[:, 8192:10240].rearrange("p (t c) -> p t c", t=NBLK),
          C.WA[:, 10240:12288].rearrange("p (t c) -> p t c", t=NBLK)]
    gb = [C.WA[:, 12288:14336], C.WA[:, 14336:16384]]
    wsb = C.BA[:, 0:4096].rearrange("p (g t) -> p g t", g=32)
    bsb = C.FA[:, 0:4096].rearrange("p (g t) -> p g t", g=32)
    ug = [C.FA[:, 4096:4608], C.FA[:, 4608:5120]]
    sv = [C.FA[:, 5120:5632], C.FA[:, 5632:6144]]
    rstd = C.rs[:, 0, 0:16]
    s_i, s_w, s_vl, s_vn, s_sv, s_u, s_a, s_d1, s_ev, s_st = (
        P.sem(tag + n) for n in ("i", "w", "vl", "vn", "sv", "u", "a", "d1", "ev", "st"))
    P.dma("gpsimd", C.BA[:, 0:4096], wsT, inc=s_i)
    P.dma("sync", bsb, bass.AP(bs.tensor, 0, [[0, 128], [128, 32], [1, 128]]), inc=s_i)
    P.wait("vector", s_i, 32)
    P.op("vector", lambda e: e.memset(wsb[64:128, :, 0:64], 0.0), inc=s_i)
    P.op("vector", lambda e: e.tensor_reduce(out=rstd, in_=ssqp[:, :, :], axis=mybir.AxisListType.X,
                                             op=ALU.add), inc=s_i)
    P.wait("scalar", s_i, 34)
    P.op("scalar", lambda e: e.activation(out=rstd, in_=rstd, func=AF.Sqrt, bias=C.epsb[:], scale=1.0 / D),
         inc=s_i)
    P.wait("vector", s_i, 35)
    P.op("vector", lambda e: e.reciprocal(out=rstd, in_=rstd), inc=s_i)
    P.wait("vector", s_i, 36)
    P.wait("tensor", s_i, 36)
    rb = rstd.unsqueeze(2).broadcast_to([128, NBLK, 128])
    for g in range(32):
        P.wait("gpsimd", s_u, 4 * (g - 1))
        P.dma("gpsimd", ub[g % 2], wu[g], inc=s_w)
        uv = ub[g % 2].rearrange("p (c j) -> p c j", c=DC)
        if g == 0:
            P.dma("sync", vn[0], vtm[0], inc=s_vl)
        P.wait("vector", s_vl, 16 * (g + 1))
        P.op("vector", lambda e, g=g: e.tensor_tensor(out=vn[g % 2], in0=vn[g % 2], in1=rb, op=ALU.mult),
             inc=s_vn)
        for tt in range(4):
            n = g * 4 + tt
            psv = C.PS[(n % 4) * 2]
            pu = C.PS[(n % 4) * 2 + 1]
            tok = slice(tt * 512, (tt + 1) * 512)
            if tt == 0:
                P.wait("tensor", s_vn, g + 1)
                P.wait("tensor", s_w, 16 * (g + 1))
            P.wait("tensor", s_ev, n - 3)
            for k in range(4):
                tb = tt * 4 + k
                fn = lambda e, k=k, tb=tb, g=g, psv=psv: e.matmul(
                    psv[:, k * 128:(k + 1) * 128], lhsT=vn[g % 2][:, tb, :], rhs=wsb[:, g, :],
                    start=True, stop=True)
                P.op("tensor", fn, inc=(s_sv if k == 3 else None))
            for c in range(DC):
                fn = lambda e, c=c, pu=pu, uv=uv, tok=tok: e.matmul(
                    pu[:], lhsT=uv[:, c, :], rhs=hT[:, c, tok], start=(c == 0), stop=(c == DC - 1))
                P.op("tensor", fn, inc=(s_u if c == DC - 1 else None))
            P.wait("scalar", s_u, n + 1)
            P.wait("scalar", s_ev, n - 1)
            P.op("scalar", lambda e, n=n, pu=pu: e.activation(out=ug[n % 2], in_=pu[:], func=AF.Gelu),
                 inc=s_a)
            P.wait("vector", s_sv, n + 1)
            bb = bsb[:, g, :].unsqueeze(1).broadcast_to([128, 4, 128])
            P.op("vector", lambda e, n=n, g=g, psv=psv, bb=bb: e.scalar_tensor_tensor(
                out=sv[n % 2].rearrange("p (k t) -> p k t", k=4),
                in0=psv[:, :].rearrange("p (k t) -> p k t", k=4),
                scalar=C.gains[:, gvcol + g:gvcol + g + 1], in1=bb, op0=ALU.mult, op1=ALU.add), inc=s_d1)
            P.wait("vector", s_d1, n + 1)
            P.wait("vector", s_a, n + 1)
            if tt == 0:
                P.wait("vector", s_st, 16 * (g - 1))
            P.op("vector", lambda e, n=n, g=g, tok=tok: e.tensor_tensor(
                out=gb[g % 2][:, tok], in0=ug[n % 2], in1=sv[n % 2], op=ALU.mult), inc=s_ev)
        if g + 1 < 32:
            P.wait("sync", s_sv, 4 * g)
            P.dma("sync", vn[(g + 1) % 2], vtm[g + 1], inc=s_vl)
        P.wait("sync", s_ev, 4 * (g + 1))
        P.dma("sync", gT[g], gb[g % 2], inc=s_st)
    return (s_st, s_st.n)


def phase_qk(P, C, wqk, qT, kT, gq_col, gk_col, tag):
    hT = C.R[:, :].rearrange("p (c t) -> p c t", c=DC)
    wb = [C.WA[:, 0:4096], C.WA[:, 4096:8192]]
    ob = [C.WA[:, 8192:10240], C.WA[:, 10240:12288]]
    sq = [C.BA[:, 0:512], C.BA[:, 512:1024]]
    rt = [C.FA[:, 0:512], C.FA[:, 512:1024]]
    s_w, s_m, s_sq, s_o, s_rt, s_rc, s_ev, s_st = (
        P.sem(tag + n) for n in ("w", "m", "sq", "o", "rt", "rc", "ev", "st"))
    NB = 64
    NG = NB * 4

    def tail(n):
        blk, tt = divmod(n, 4)
        pm = C.PS[(n % 4) * 2]
        pq = C.PS[(n % 4) * 2 + 1]
        tok = slice(tt * 512, (tt + 1) * 512)
        col = (gq_col if blk < 32 else gk_col)
        P.wait("tensor", s_sq, n + 1)
        P.op("tensor", lambda e: e.matmul(pq[:], lhsT=C.ones128[:], rhs=sq[n % 2], start=True, stop=True),
             inc=s_o)
        P.wait("scalar", s_o, n + 1)
        P.op("scalar", lambda e: e.activation(out=rt[n % 2], in_=pq[:], func=AF.Sqrt, bias=C.epsb[:],
                                              scale=1.0), inc=s_rt)
        P.wait("vector", s_rt, n + 1)
        P.op("vector", lambda e: e.reciprocal(out=rt[n % 2], in_=rt[n % 2]), inc=s_rc)
        P.wait("vector", s_rc, n + 1)
        if tt == 0:
            P.wait("vector", s_st, 16 * (blk - 1))
        P.op("vector", lambda e: e.scalar_tensor_tensor(
            out=ob[blk % 2][:, tok], in0=pm[:], scalar=C.gains[:, col:col + 1], in1=rt[n % 2],
            op0=ALU.mult, op1=ALU.mult), inc=s_ev)
        if tt == 3:
            P.wait("sync", s_ev, 4 * (blk + 1))
            dst = qT[blk] if blk < 32 else kT[blk - 32]
            P.dma("sync", dst, ob[blk % 2], inc=s_st)

    for blk in range(NB):
        P.wait("gpsimd", s_m, 4 * (blk - 1))
        P.dma("gpsimd", wb[blk % 2], wqk[blk], inc=s_w)
        wv = wb[blk % 2].rearrange("p (c j) -> p c j", c=DC)
        for tt in range(4):
            n = blk * 4 + tt
            pm = C.PS[(n % 4) * 2]
            tok = slice(tt * 512, (tt + 1) * 512)
            if tt == 0:
                P.wait("tensor", s_w, 16 * (blk + 1))
            P.wait("tensor", s_ev, n - 3)
            for c in range(DC):
                fn = lambda e, c=c, pm=pm, wv=wv, tok=tok: e.matmul(
                    pm[:], lhsT=wv[:, c, :], rhs=hT[:, c, tok], start=(c == 0), stop=(c == DC - 1))
                P.op("tensor", fn, inc=(s_m if c == DC - 1 else None))
            P.wait("scalar", s_m, n + 1)
            P.wait("scalar", s_o, n - 1)
            P.op("scalar", lambda e, n=n, pm=pm: e.activation(out=sq[n % 2], in_=pm[:], func=AF.Square),
                 inc=s_sq)
            if n >= 1:
                tail(n - 1)
    tail(NG - 1)
    return (s_st, s_st.n)


def attn_iters():
    its = []
    for h in range(32):
        for j in range(4):
            for kb in range(8 * j + 7, -1, -1):
                its.append((h, j, kb))
    return its


def phase_attn(P, C, qT, kTg, vg, masks_sb, tag):
    oT = C.R[:, :].rearrange("p (c t) -> p c t", c=DC)
    kb_ = [C.WA[:, 0:4096].rearrange("p (r t) -> p r t", r=2),
           C.WA[:, 8192:12288].rearrange("p (r t) -> p r t", r=2)]
    vb_ = [C.WA[:, 4096:8192].rearrange("p (r i d) -> p r i d", r=2, i=NBLK),
           C.WA[:, 12288:16384].rearrange("p (r i d) -> p r i d", r=2, i=NBLK)]
    qb_ = [C.BA[:, 0:2048], C.BA[:, 2048:4096]]
    NE = 4
    eb = [C.FA[:, k * 512:(k + 1) * 512] for k in range(NE)]
    af = [C.FA[:, 2048 + k * 512:2048 + (k + 1) * 512] for k in range(2)]
    wk = C.FA[:, 6144:7424].bitcast(BF16)
    spb = [wk[:, k * 512:(k + 1) * 512] for k in range(3)]
    ab = [wk[:, 1536 + k * 512:1536 + (k + 1) * 512] for k in range(2)]
    wk2 = C.FA[:, 7424:7936].bitcast(BF16)
    acc = [wk2[:, k * 512:(k + 1) * 512] for k in range(2)]
    PZ = [C.PS[0], C.PS[1], C.PS[2]]
    PL = [C.PS[3], C.PS[4]]
    PO = [C.PS[5], C.PS[6]]
    s_hl, s_z, s_e, s_sp, s_la, s_acc, s_a, s_mul, s_av, s_oe = (
        P.sem(tag + n) for n in ("hl", "z", "e", "sp", "la", "acc", "a", "mul", "av", "oe"))
    its = attn_iters()
    N = len(its)
    kqv = [kTg[h // 2, :, h % 2].rearrange("r p t -> p r t") for h in range(32)]
    vv = [vg[h // 2, :, h % 2].rearrange("r p i d -> p r i d") for h in range(32)]
    head_end = {h: 80 * (h + 1) for h in range(32)}

    def c0_of(o):
        return {7: 3, 6: 3, 5: 2, 4: 2, 3: 1, 2: 1}.get(o, 0)

    def meta(i):
        h, j, kb = its[i]
        o = kb - 8 * j
        first = (o == 7)
        last = (kb == 0)
        r = (kb & 1) ^ ((kb >> 1) & 1)
        il = kb >> 1
        tidx = h * 4 + j
        c0 = c0_of(o) * 128
        cp = c0_of(o + 1) * 128 if not first else 512
        return h, j, kb, o, first, last, r, il, tidx, c0, cp

    def load_head(h):
        if h >= 2:
            P.wait("sync", s_av, head_end[h - 2])
        P.dma("sync", kb_[h % 2], kqv[h], inc=s_hl)
        P.dma("sync", vb_[h % 2], vv[h], inc=s_hl)
        P.dma("sync", qb_[h % 2], qT[h], inc=s_hl)

    def Z(i):
        h, j, kb, o, first, last, r, il, tidx, c0, cp = meta(i)
        if i % 80 == 0:
            P.wait("tensor", s_hl, 48 * (h + 1))
        P.wait("tensor", s_e, i - 2)
        pz = PZ[i % 3]
        hm = (o >= 0)
        P.op("tensor", lambda e: e.matmul(pz[:, c0:512], lhsT=kb_[h % 2][:, r, il * 128:(il + 1) * 128],
                                          rhs=qb_[h % 2][:, j * 512 + c0:(j + 1) * 512], start=True, stop=not hm),
             inc=(None if hm else s_z))
        if hm:
            P.op("tensor", lambda e: e.matmul(pz[:, c0:512], lhsT=C.ident[:], rhs=masks_sb[:, o, c0:512],
                                              start=False, stop=True), inc=s_z)

    def E(i):
        c0 = meta(i)[9]
        P.wait("scalar", s_z, i + 1)
        P.wait("scalar", s_mul, i - NE + 1)
        pz = PZ[i % 3]
        P.op("scalar", lambda e: e.activation(out=eb[i % NE][:, c0:512], in_=pz[:, c0:512], func=AF.Exp), inc=s_e)

    def SP(i):
        c0 = meta(i)[9]
        P.wait("scalar", s_e, i + 1)
        P.wait("scalar", s_la, i - 2)
        P.wait("scalar", s_acc, i - 2)
        P.op("scalar", lambda e: e.activation(out=spb[i % 3][:, c0:512], in_=eb[i % NE][:, c0:512], func=AF.Ln,
                                              bias=1.0, scale=1.0), inc=s_sp)

    def LA(i):
        h, j, kb, o, first, last, r, il, tidx, c0, cp = meta(i)
        P.wait("tensor", s_sp, i + 1)
        if not first:
            P.wait("tensor", s_acc, i)
        P.wait("tensor", s_a, i - 1)
        pl = PL[i % 2]
        P.op("tensor", lambda e: e.matmul(pl[:, c0:512], lhsT=C.negtri[:], rhs=spb[i % 3][:, c0:512],
                                          start=True, stop=first), inc=(s_la if first else None))
        if not first:
            P.op("tensor", lambda e: e.matmul(pl[:, cp:512], lhsT=C.negones[:], rhs=acc[tidx % 2][:, cp:512],
                                              start=False, stop=True), inc=s_la)

    def ACC(i):
        h, j, kb, o, first, last, r, il, tidx, c0, cp = meta(i)
        P.wait("vector", s_sp, i + 1)
        P.wait("vector", s_la, i + 1)
        if not first:
            P.wait("vector", s_acc, i)
        if c0 < cp:
            P.op("vector", lambda e: e.tensor_copy(out=acc[tidx % 2][:, c0:cp], in_=spb[i % 3][:, c0:cp]),
                 inc=(s_acc if first else None))
        if not first:
            P.op("vector", lambda e: e.tensor_tensor(out=acc[tidx % 2][:, cp:512], in0=acc[tidx % 2][:, cp:512],
                                                     in1=spb[i % 3][:, cp:512], op=ALU.add), inc=s_acc)

    def A(i):
        c0 = meta(i)[9]
        P.wait("scalar", s_la, i + 1)
        P.wait("scalar", s_mul, i - 1)
        pl = PL[i % 2]
        P.op("scalar", lambda e: e.activation(out=af[i % 2][:, c0:512], in_=pl[:, c0:512], func=AF.Exp), inc=s_a)

    def MUL(i):
        c0 = meta(i)[9]
        P.wait("vector", s_a, i + 1)
        P.wait("vector", s_av, i - 1)
        P.op("vector", lambda e: e.tensor_tensor(out=ab[i % 2][:, c0:512], in0=af[i % 2][:, c0:512],
                                                 in1=eb[i % NE][:, c0:512], op=ALU.mult), inc=s_mul)

    def AV(i):
        h, j, kb, o, first, last, r, il, tidx, c0, cp = meta(i)
        P.wait("tensor", s_mul, i + 1)
        if first:
            P.wait("tensor", s_oe, tidx - 1)
        po = PO[tidx % 2]
        P.op("tensor", lambda e: e.matmul(po[:, c0:512], lhsT=vb_[h % 2][:, r, il, :], rhs=ab[i % 2][:, c0:512],
                                          start=first, stop=last), inc=s_av)
        if last:
            P.wait("vector", s_av, i + 1)
            P.op("vector", lambda e: e.tensor_copy(out=oT[:, h, j * 512:(j + 1) * 512], in_=po[:]), inc=s_oe)

    load_head(0)
    load_head(1)
    for s in range(-1, N + 2):
        if 0 <= s + 1 < N:
            h = its[s + 1][0]
            if (s + 1) % 80 == 0 and h >= 1 and h + 1 < 32:
                load_head(h + 1)
            Z(s + 1)
            E(s + 1)
        if 0 <= s < N:
            SP(s)
        if 0 <= s - 1 < N:
            LA(s - 1)
            A(s - 1)
            ACC(s - 1)
            MUL(s - 1)
        if 0 <= s - 2 < N:
            AV(s - 2)
    return (s_oe, 128)


def ffn_layer(P, C, gcol, wgu, wd, actT, xsrc, xdst, tag):
    d = phase_norm(P, C, xsrc, gcol, tag + "n")
    barrier(P, [d])
    d = phase_ffn_g(P, C, wgu, actT, tag + "g")
    barrier(P, [d])
    d = phase_proj_res(P, C, wd, FH, 2, 2, actT, xsrc, xdst, tag + "d")
    barrier(P, [d])
    return d


def gmlp_layer(P, C, j, W, S, xsrc, xdst, tag):
    d = phase_norm(P, C, xsrc, GCOL["a_norm"][j], tag + "n")
    barrier(P, [d])
    d = phase_tm(P, C, W[f"a_wv{j}"], S["vtm"], "gelu", C.ssqp, tag + "v")
    barrier(P, [d])
    d = phase_gmlp_gate(P, C, W[f"a_wu{j}"], W[f"a_ws{j}"], W[f"a_bs{j}"], S["vtm"], S["gT"],
                        GCOL["a_vn"][j], C.ssqp, tag + "g")
    barrier(P, [d])
    d = phase_proj_res(P, C, W[f"a_wo{j}"], DC, 1, 1, S["gT"], xsrc, xdst, tag + "o")
    barrier(P, [d])
    return d


def bproj_layer(P, C, j, W, S, xsrc, qT, kT, vl, tag):
    d = phase_norm(P, C, xsrc, GCOL["b_norm"][j], tag + "n")
    barrier(P, [d])
    d = phase_qk(P, C, W[f"b_wqk{j}"], qT, kT, GCOL["q"][j], GCOL["k"][j], tag + "q")
    barrier(P, [d])
    d = phase_tm(P, C, W[f"b_wv{j}"], vl, "copy", None, tag + "v")
    barrier(P, [d])
    return d


def battn_layer(P, C, j, W, S, masks, qT, kTg, vg, xsrc, xdst, tag):
    msb = C.FA[:, 4096:6144].bitcast(BF16).rearrange("p (o t) -> p o t", o=8)
    sm = P.sem(tag + "mk")
    P.dma("gpsimd", C.FA[:, 4096:6144].bitcast(BF16), masks, inc=sm)
    barrier(P, [(sm, 16)])
    d = phase_attn(P, C, qT, kTg, vg, msb, tag + "a")
    barrier(P, [d])
    d = phase_proj_res(P, C, W[f"b_wo{j}"], DC, 1, 1, None, xsrc, xdst, tag + "o")
    barrier(P, [d])
    return d


WSHAPES = {}
for _j in range(2):
    WSHAPES[f"a_wv{_j}"] = [16, 128, 8192]
    WSHAPES[f"a_wu{_j}"] = [32, 128, 4096]
    WSHAPES[f"a_ws{_j}"] = [128, 4096]
    WSHAPES[f"a_bs{_j}"] = [32, 128]
    WSHAPES[f"a_wo{_j}"] = [1, 32, 128, 4096]
    WSHAPES[f"b_wqk{_j}"] = [64, 128, 4096]
    WSHAPES[f"b_wv{_j}"] = [16, 128, 8192]
    WSHAPES[f"b_wo{_j}"] = [1, 32, 128, 4096]
for _l in range(4):
    WSHAPES[f"wgu{_l}"] = [FC, 128, 8192]
    WSHAPES[f"wd{_l}"] = [2, DC, 128, FH * 128]

SEG_STEPS = {
    0: ["A0", "F0", "P0"],
    1: ["T0", "F1", "A1", "F2", "P1"],
    2: ["T1", "F3"],
    "t": ["T0"],
    "all": ["A0", "F0", "P0", "G0", "T0", "F1", "A1", "F2", "P1", "G1", "T1", "F3"],
}
STEP_W = {"A0": ["a_wv0", "a_wu0", "a_ws0", "a_bs0", "a_wo0"], "A1": ["a_wv1", "a_wu1", "a_ws1", "a_bs1", "a_wo1"],
          "P0": ["b_wqk0", "b_wv0"], "P1": ["b_wqk1", "b_wv1"], "T0": ["b_wo0"], "T1": ["b_wo1"],
          "F0": ["wgu0", "wd0"], "F1": ["wgu1", "wd1"], "F2": ["wgu2", "wd2"], "F3": ["wgu3", "wd3"],
          "G0": [], "G1": []}
PAIRS = [[0, 1], [2, 3], [4, 5], [6, 7]]


def seg_weight_names(seg):
    out = []
    for st in SEG_STEPS[seg]:
        out += STEP_W[st]
    return out


def build_program(seg):
    nc = bass.Bass("TRN2", target_bir_lowering=False)
    steps = SEG_STEPS[seg]
    fused = (seg == "all")
    W = {}
    for name in seg_weight_names(seg):
        W[name] = nc.dram_tensor(name, WSHAPES[name], F32, kind="ExternalInput").ap()
    gains = nc.dram_tensor("gains", [128, NGAIN], F32, kind="ExternalInput").ap()
    consts = nc.dram_tensor("consts", [128, 3, 128], F32, kind="ExternalInput").ap()
    masks = nc.dram_tensor("masks", [128, 4096], F32, kind="ExternalInput").ap()
    xin = nc.dram_tensor("xin", [DC, 128, T], F32, kind="ExternalInput").ap()
    xo = nc.dram_tensor("xo", [DC, 128, T], F32, kind="ExternalOutput").ap()
    S = {}
    S["actT"] = nc.dram_tensor("actT", [FC, 128, T], BF16, kind="Internal").ap()
    S["vtm"] = nc.dram_tensor("vtm", [32, 128, NBLK, 128], BF16, kind="Internal").ap()
    S["gT"] = nc.dram_tensor("gT", [32, 128, T], BF16, kind="Internal").ap()
    if fused:
        qT = nc.dram_tensor("qT", [32, 128, T], BF16, kind="Internal").ap()
        kT = nc.dram_tensor("kT", [32, 128, T], BF16, kind="Internal").ap()
        vl = nc.dram_tensor("vl", [32, 128, NBLK, 128], BF16, kind="Internal").ap()
        kTgs = [nc.dram_tensor(f"kTg{i}", [16, 2, 2, 128, T], BF16, kind="Internal").ap() for i in range(2)]
        vgs = [nc.dram_tensor(f"vg{i}", [16, 2, 2, 128, NBLK, 128], BF16, kind="Internal").ap() for i in range(2)]
        q_in = qT
    else:
        if seg in (0, 1):
            qT = nc.dram_tensor("qTo", [32, 128, T], BF16, kind="ExternalOutput").ap()
            kT = nc.dram_tensor("kTo", [32, 128, T], BF16, kind="ExternalOutput").ap()
            vl = nc.dram_tensor("vlo", [32, 128, NBLK, 128], BF16, kind="ExternalOutput").ap()
        if seg in (1, 2, "t"):
            q_in = nc.dram_tensor("qin", [32, 128, T], BF16, kind="ExternalInput").ap()
            kTg = nc.dram_tensor("kTg", [16, 2, 2, 128, T], BF16, kind="ExternalInput").ap()
            vg = nc.dram_tensor("vg", [16, 2, 2, 128, NBLK, 128], BF16, kind="ExternalInput").ap()
    with ExitStack() as st:
        P = Prog(nc, st)
        C = setup_ctx(P, gains[:, :], NGAIN, consts[:, :, :])
        xcur = xin
        d = None
        for step in steps:
            kind, idx = step[0], int(step[1])
            tag = step.lower()
            if kind == "A":
                d = gmlp_layer(P, C, idx, W, S, xcur, xo, tag)
                xcur = xo
            elif kind == "F":
                d = ffn_layer(P, C, GCOL["f_norm"][idx], W[f"wgu{idx}"], W[f"wd{idx}"], S["actT"], xcur, xo, tag)
                xcur = xo
            elif kind == "P":
                d = bproj_layer(P, C, idx, W, S, xcur, qT, kT, vl, tag)
            elif kind == "G":
                kTg, vg = kTgs[idx], vgs[idx]
                sg = P.sem(tag + "cc")
                k2 = kT.rearrange("h p t -> (h p) t")
                v2 = vl.rearrange("h p i d -> (h p) (i d)")
                pairs = []
                for ck in range(16):
                    pairs.append((k2[ck * 256:(ck + 1) * 256, :], kTg[ck].rearrange("r h p t -> (r h p) t")))
                    pairs.append((v2[ck * 256:(ck + 1) * 256, :], vg[ck].rearrange("r h p i d -> (r h p) (i d)")))
                for (src, dst) in pairs:
                    def fn(e, src=src, dst=dst):
                        return e.collective_compute("AllGather", ALU.bypass, replica_groups=PAIRS,
                                                    ins=[src.opt()], outs=[dst.opt()])
                    P.op("gpsimd", fn, inc=sg)
                d = (sg, sg.n)
                barrier(P, [d])
            elif kind == "T":
                if fused:
                    kTg, vg = kTgs[idx], vgs[idx]
                d = battn_layer(P, C, idx, W, S, masks, q_in, kTg, vg, xcur, xo, tag)
                xcur = xo
        barrier(P, [d])
        P.emit()
    return nc


def host_layout_ffn(w_gate, w_up, w_down):
    g = w_gate.reshape(DC, 128, FC, 128).transpose(2, 1, 0, 3)
    u = w_up.reshape(DC, 128, FC, 128).transpose(2, 1, 0, 3)
    wgu = np.empty((FC, 128, 2, DC, 128), np.float32)
    wgu[:, :, 0] = g
    wgu[:, :, 1] = u
    wgu = wgu.reshape(FC, 128, 2 * DC * 128)
    wd = np.ascontiguousarray(
        w_down.reshape(2, FH, 128, DC, 128).transpose(0, 3, 2, 1, 4)).reshape(2, DC, 128, FH * 128)
    return wgu, wd


def gain_cols(g):
    return np.ascontiguousarray(g.reshape(DC, 128).T)


def block_perm(r):
    return [2 * i + (r ^ (i & 1)) for i in range(NBLK)]


def x_to_core(x):
    outs = []
    for c in range(NCORES):
        b, r = c // 2, c % 2
        xb = x[b].reshape(SEQ // 128, 128, D)[block_perm(r)].reshape(T, D)
        outs.append(np.ascontiguousarray(xb.T).reshape(DC, 128, T))
    return outs


def core_to_x(outs):
    y = np.empty((BATCH, SEQ, D), np.float32)
    for c in range(NCORES):
        b, r = c // 2, c % 2
        xt = outs[c].reshape(D, T).T.reshape(NBLK, 128, D)
        yb = y[b].reshape(SEQ // 128, 128, D)
        yb[block_perm(r)] = xt
    return y


def tile_cols(w, nblk, width):
    return np.ascontiguousarray(
        w.reshape(DC, 128, nblk, width).transpose(2, 1, 0, 3)).reshape(nblk, 128, DC * width)


def host_weights(inp, names):
    Wd = {}
    for name in names:
        base, j = name[:-1], int(name[-1])
        if base == "a_wv":
            Wd[name] = tile_cols(inp["a_w_in"][j][:, 4096:], 16, 256)
        elif base == "a_wu":
            Wd[name] = tile_cols(inp["a_w_in"][j][:, :4096], 32, 128)
        elif base == "a_ws":
            Wd[name] = np.ascontiguousarray(inp["a_w_s"][j].transpose(2, 0, 1)).reshape(128, 4096)
        elif base == "a_bs":
            Wd[name] = np.ascontiguousarray(inp["a_b_s"][j])
        elif base == "a_wo":
            Wd[name] = tile_cols(inp["a_w_out"][j], 32, 128).reshape(1, 32, 128, 4096)
        elif base == "b_wqk":
            Wd[name] = tile_cols(inp["b_w_qkv"][j][:, :8192], 64, 128)
        elif base == "b_wv":
            Wd[name] = tile_cols(inp["b_w_qkv"][j][:, 8192:], 16, 256)
        elif base == "b_wo":
            Wd[name] = tile_cols(inp["b_w_out"][j], 32, 128).reshape(1, 32, 128, 4096)
        elif base == "wgu":
            Wd[name], Wd[f"wd{j}"] = host_layout_ffn(inp["f_w_gate"][j], inp["f_w_up"][j], inp["f_w_down"][j])
        elif base == "wd":
            pass
        else:
            raise KeyError(name)
    return Wd


def host_gains(inp):
    g = np.zeros((128, NGAIN), np.float32)
    for j in range(2):
        g[:, GCOL["a_norm"][j]:GCOL["a_norm"][j] + 32] = gain_cols(inp["a_norm"][j])
        g[:, GCOL["a_vn"][j]:GCOL["a_vn"][j] + 32] = gain_cols(inp["a_v_norm"][j])
        g[:, GCOL["b_norm"][j]:GCOL["b_norm"][j] + 32] = gain_cols(inp["b_norm"][j])
        g[:, GCOL["q"][j]] = inp["b_q_norm"][j]
        g[:, GCOL["k"][j]] = inp["b_k_norm"][j]
    for i in range(4):
        g[:, GCOL["f_norm"][i]:GCOL["f_norm"][i] + 32] = gain_cols(inp["f_norm"][i])
    return g


def host_consts():
    c = np.zeros((128, 3, 128), np.float32)
    c[:, 0, :] = np.eye(128, dtype=np.float32)
    jj, ss = np.meshgrid(np.arange(128), np.arange(128), indexing="ij")
    c[:, 1, :] = np.where(jj >= ss, -1.0, 0.0)
    c[:, 2, :] = -1.0
    return c


def host_masks(r):
    rel = [i * 2 + (r ^ (i & 1)) for i in range(4)]
    m = np.zeros((128, 8, 4, 128), np.float32)
    ss, tt = np.meshgrid(np.arange(128), np.arange(128), indexing="ij")
    diag = np.where(ss >= tt, NEG, 0.0).astype(np.float32)
    for o in range(8):
        for i in range(4):
            if rel[i] == o:
                m[:, o, i, :] = diag
            elif rel[i] < o:
                m[:, o, i, :] = NEG
    return m.reshape(128, 4096)


_PROG_CACHE = {}


def _get_prog(seg):
    if seg not in _PROG_CACHE:
        _PROG_CACHE[seg] = build_program(seg)
    return _PROG_CACHE[seg]


FUSED = True


def _pair_gather(arrs):
    out = []
    for c in range(NCORES):
        p = (c // 2) * 2
        a = arrs[p].reshape((16, 2) + arrs[p].shape[1:])
        b = arrs[p + 1].reshape((16, 2) + arrs[p + 1].shape[1:])
        out.append(np.stack([a, b], axis=1))
    return out


def kernel(**inputs):
    inp = {k: np.asarray(v) for k, v in inputs.items()}
    xs = x_to_core(inp["x"].astype(np.float32, copy=False))
    gains = host_gains(inp)
    consts = host_consts()
    masks = [host_masks(c % 2) for c in range(NCORES)]
    cores = list(range(NCORES))
    if FUSED:
        Wd = host_weights(inp, seg_weight_names("all"))
        ims = []
        for c in cores:
            im = {"xin": xs[c], "gains": gains, "consts": consts, "masks": masks[c]}
            im.update(Wd)
            ims.append(im)
        res = run_bass_kernel_spmd(_get_prog("all"), ims, core_ids=cores)
        return core_to_x([res.results[c]["xo"] for c in cores])
    xcur = xs
    q = kg = vg = None
    for seg in (0, 1, 2):
        Wd = host_weights(inp, seg_weight_names(seg))
        ims = []
        for c in cores:
            im = {"xin": xcur[c], "gains": gains, "consts": consts, "masks": masks[c]}
            if seg >= 1:
                im.update({"qin": q[c], "kTg": kg[c], "vg": vg[c]})
            im.update(Wd)
            ims.append(im)
        res = run_bass_kernel_spmd(_get_prog(seg), ims, core_ids=cores)
        del ims, Wd
        xcur = [res.results[c]["xo"] for c in cores]
        if seg < 2:
            q = [res.results[c]["qTo"] for c in cores]
            kg = _pair_gather([res.results[c]["kTo"] for c in cores])
            vg = _pair_gather([res.results[c]["vlo"] for c in cores])
    return core_to_x(xcur)
```

```python
import math
from contextlib import ExitStack

import numpy as np
import ml_dtypes

import concourse.bass as bass
import concourse.mybir as mybir
from concourse.bass_utils import run_bass_kernel_spmd

F32 = mybir.dt.float32
BF16 = mybir.dt.bfloat16
AF = mybir.ActivationFunctionType
ALU = mybir.AluOpType

D = 4096
DC = 32
SEQ = 4096
BATCH = 4
T = 2048
NBLK = 16
F = 11008
FC = 86
FH = 43
DEPTH = 4
EPS = 1e-6
NCORES = 8
NEG = -30000.0
NCH = 8
HPC = 32 // NCH


class PhysSem:
    def __init__(self, h):
        self.h = h
        self.count = 0


class Sem:
    def __init__(self, phys, gen):
        self.phys = phys
        self.h = phys.h
        self.base = phys.count
        self.n = 0
        self.gen = gen


class Prog:
    def __init__(self, nc, stack):
        self.nc = nc
        self.stack = stack
        self.q = {e: [] for e in ("sync", "scalar", "vector", "gpsimd", "tensor")}
        self.nsem = 0
        self.gen = 0
        self.free = []
        self.live = []

    def sem(self, name):
        if self.free:
            phys = self.free.pop()
        else:
            self.nsem += 1
            phys = PhysSem(self.stack.enter_context(self.nc.semaphore(f"sem_{self.nsem}")))
        v = Sem(phys, self.gen)
        self.live.append(v)
        return v

    def next_gen(self):
        self.gen += 1
        keep = []
        for v in self.live:
            if v.gen <= self.gen - 2:
                v.phys.count = v.base + v.n
                self.free.append(v.phys)
            else:
                keep.append(v)
        self.live = keep

    def sb(self, name, shape, dt):
        return self.stack.enter_context(self.nc.sbuf_tensor(name, shape, dt))

    def psum(self, name, shape, dt):
        return self.stack.enter_context(self.nc.psum_tensor(name, shape, dt))

    def op(self, eng, fn, inc=None):
        if inc is None:
            self.q[eng].append(fn)
            return None
        step = 16 if getattr(fn, "_is_dma", False) else 1
        inc.n += step
        h = inc.h

        def f(e, fn=fn, h=h, step=step):
            fn(e).then_inc(h, step)
        self.q[eng].append(f)
        return inc.n

    def dma(self, eng, out, in_, inc=None):
        def fn(e, out=out, in_=in_):
            return e.dma_start(out=out, in_=in_)
        fn._is_dma = True
        return self.op(eng, fn, inc)

    def wait(self, eng, sem, val):
        if val is None or val <= 0:
            return
        h = sem.h
        val = val + sem.base
        self.q[eng].append(lambda e, h=h, val=val: e.wait_ge(h, val))

    def emit(self):
        nc = self.nc
        with nc.Block() as block:
            @block.sync
            def _(e):
                for f in self.q["sync"]:
                    f(e)

            @block.scalar
            def _(e):
                for f in self.q["scalar"]:
                    f(e)

            @block.vector
            def _(e):
                for f in self.q["vector"]:
                    f(e)

            @block.gpsimd
            def _(e):
                for f in self.q["gpsimd"]:
                    f(e)

            @block.tensor
            def _(e):
                for f in self.q["tensor"]:
                    f(e)


class Ctx:
    pass


GCOL = {"a_norm": (0, 32), "a_vn": (64, 96), "b_norm": (128, 160), "f_norm": (192, 224, 256, 288),
        "q": (320, 321), "k": (322, 323)}
NGAIN = 324


def setup_ctx(P, gains_ap, ngain, consts_ap=None):
    nc = P.nc
    C = Ctx()
    C.R = P.sb("R", [128, 65536], BF16)
    C.WA = P.sb("WA", [128, 16384], BF16)
    C.FA = P.sb("FA", [128, 8192], F32)
    C.BA = P.sb("BA", [128, 4096], BF16)
    C.gains = P.sb("gains_sb", [128, ngain], F32)
    C.ones = P.sb("ones", [128, 128], BF16)
    C.ones128 = P.sb("ones128", [128, 128], BF16)
    C.rs = P.sb("rs", [128, 2, 128], F32)
    C.epsb = P.sb("epsb", [128, 1], F32)
    C.PS = [P.psum(f"ps{i}", [128, 512], F32) for i in range(8)]
    C.init = P.sem("init")
    P.op("vector", lambda e: e.memset(C.ones[:], 1.0 / D))
    P.op("vector", lambda e: e.memset(C.ones128[:], 1.0 / 128))
    P.op("vector", lambda e: e.memset(C.epsb[:], EPS), inc=C.init)
    v = P.dma("sync", C.gains[:], gains_ap, inc=C.init)
    if consts_ap is not None:
        C.cst = P.sb("cst", [128, 3, 128], BF16)
        C.ident = C.cst[:, 0, :]
        C.negtri = C.cst[:, 1, :]
        C.negones = C.cst[:, 2, :]
        C.ssqp = P.sb("ssqp", [128, 16, 16], F32)
        P.dma("gpsimd", C.cst[:], consts_ap, inc=C.init)
        P.wait("vector", C.init, C.init.n)
        for col in GCOL["q"]:
            P.op("vector", lambda e, col=col: e.tensor_scalar(
                out=C.gains[:, col:col + 1], in0=C.gains[:, col:col + 1], scalar1=1.0 / math.sqrt(128.0),
                scalar2=None, op0=ALU.mult), inc=C.init)
    C.init_val = C.init.n
    for eng in ("scalar", "vector", "tensor", "gpsimd"):
        P.wait(eng, C.init, C.init_val)
    return C


def barrier(P, dones):
    for eng in ("sync", "scalar", "vector", "gpsimd", "tensor"):
        for (s, v) in dones:
            P.wait(eng, s, v)
    P.next_gen()


def phase_norm(P, C, xsrc, gcol, tag):
    hT = C.R[:, :].rearrange("p (c t) -> p c t", c=DC)
    xv = xsrc.rearrange("c p t -> p c t")
    xt = [C.FA[:, 0:4096].rearrange("p (c t) -> p c t", c=DC),
          C.FA[:, 4096:8192].rearrange("p (c t) -> p c t", c=DC)]
    sq = C.BA[:, 0:4096].rearrange("p (c t) -> p c t", c=DC)
    s_ld, s_sq, s_ss, s_rt, s_dv = (P.sem(tag + n) for n in ("ld", "sq", "ss", "rt", "dv"))
    g_b = C.gains[:, gcol:gcol + DC].unsqueeze(2).broadcast_to([128, DC, 128])
    for i in range(NBLK):
        b = i % 2
        tok = slice(i * 128, (i + 1) * 128)
        P.wait("sync", s_dv, 3 * (i - 1))
        P.dma("sync", xt[b], xv[:, :, tok], inc=s_ld)
        P.wait("scalar", s_ld, 16 * (i + 1))
        P.wait("scalar", s_ss, i)
        P.op("scalar", lambda e, b=b: e.activation(out=sq, in_=xt[b], func=AF.Square), inc=s_sq)
        P.wait("tensor", s_sq, i + 1)
        P.wait("tensor", s_rt, i - 1)
        ps = C.PS[b][:, 0:128]
        for c in range(DC):
            fn = lambda e, c=c, ps=ps: e.matmul(ps, lhsT=C.ones[:], rhs=sq[:, c, :],
                                                start=(c == 0), stop=(c == DC - 1))
            P.op("tensor", fn, inc=(s_ss if c == DC - 1 else None))
        P.wait("scalar", s_ss, i + 1)
        P.wait("scalar", s_dv, 3 * (i - 1))
        P.op("scalar", lambda e, b=b, ps=ps: e.activation(out=C.rs[:, b, :], in_=ps, func=AF.Sqrt,
                                                          bias=C.epsb[:], scale=1.0), inc=s_rt)
        P.wait("vector", s_rt, i + 1)
        P.op("vector", lambda e, b=b: e.reciprocal(out=C.rs[:, b, :], in_=C.rs[:, b, :]), inc=s_dv)
        P.wait("vector", s_dv, 3 * i + 1)
        for c in range(DC):
            P.op("vector", lambda e, b=b, tok=tok, c=c: e.scalar_tensor_tensor(
                out=hT[:, c, tok], in0=xt[b][:, c, :], scalar=C.gains[:, gcol + c:gcol + c + 1],
                in1=C.rs[:, b, :], op0=ALU.mult, op1=ALU.mult),
                inc=(s_dv if c in (DC // 2 - 1, DC - 1) else None))
    return (s_dv, 3 * NBLK)


def phase_ffn_g(P, C, wgu, actT, tag):
    hT = C.R[:, :].rearrange("p (c t) -> p c t", c=DC)
    wb = [C.WA[:, 0:8192], C.WA[:, 8192:16384]]
    sg = [C.FA[:, 0:512], C.FA[:, 512:1024]]
    ab = [C.BA[:, 0:2048], C.BA[:, 2048:4096]]
    s_w, s_g, s_u, s_a, s_d, s_st = (P.sem(tag + n) for n in ("w", "g", "u", "a", "d", "st"))
    for fc in range(FC):
        wbuf = wb[fc % 2]
        P.wait("gpsimd", s_u, 4 * (fc - 1))
        P.dma("gpsimd", wbuf, wgu[fc], inc=s_w)
        wv = wbuf.rearrange("p (s c j) -> p s c j", s=2, c=DC)
        for tt in range(4):
            gi = fc * 4 + tt
            pg = C.PS[(gi % 4) * 2]
            pu = C.PS[(gi % 4) * 2 + 1]
            tok = slice(tt * 512, (tt + 1) * 512)
            if tt == 0:
                P.wait("tensor", s_w, 16 * (fc + 1))
            P.wait("tensor", s_d, gi - 3)
            for s, pp, sem in ((0, pg, s_g), (1, pu, s_u)):
                for c in range(DC):
                    fn = lambda e, s=s, c=c, pp=pp, wv=wv, tok=tok: e.matmul(
                        pp[:], lhsT=wv[:, s, c, :], rhs=hT[:, c, tok], start=(c == 0), stop=(c == DC - 1))
                    P.op("tensor", fn, inc=(sem if c == DC - 1 else None))
            P.wait("scalar", s_g, gi + 1)
            P.wait("scalar", s_d, gi - 1)
            P.op("scalar", lambda e, gi=gi, pg=pg: e.activation(out=sg[gi % 2], in_=pg[:], func=AF.Silu),
                 inc=s_a)
            P.wait("vector", s_a, gi + 1)
            P.wait("vector", s_u, gi + 1)
            if tt == 0:
                P.wait("vector", s_st, 16 * (fc - 1))
            P.op("vector", lambda e, gi=gi, pu=pu, fc=fc, tok=tok: e.tensor_tensor(
                out=ab[fc % 2][:, tok], in0=sg[gi % 2], in1=pu[:], op=ALU.mult), inc=s_d)
        P.wait("sync", s_d, 4 * (fc + 1))
        P.dma("sync", actT[fc], ab[fc % 2], inc=s_st)
    return (s_st, 16 * FC)


def phase_proj_res(P, C, wd, KC, NTH, NFH, actT, xsrc, xdst, tag):
    TH = T // NTH
    NT2 = TH // 512
    Rv = C.R[:, 0:KC * TH].rearrange("p (c t) -> p c t", c=KC)
    wb = [C.WA[:, 0:KC * 128], C.WA[:, 8192:8192 + KC * 128]]
    xs = [C.FA[:, k * 512:(k + 1) * 512] for k in range(4)]
    s_al, s_w, s_m, s_xl, s_dv, s_st = (P.sem(tag + n) for n in ("al", "w", "m", "xl", "dv", "st"))
    groups = []
    for th in range(NTH):
        for fh in range(NFH):
            for dc in range(DC):
                for t2 in range(NT2):
                    groups.append((th, fh, dc, t2))
    NG = len(groups)
    store_val = {}

    def issue_xload(n):
        th, fh, dc, t2 = groups[n]
        tok = slice(th * TH + t2 * 512, th * TH + (t2 + 1) * 512)
        P.wait("sync", s_st, 16 * (n - 3))
        if fh >= 1:
            P.wait("sync", s_st, store_val[(th, fh - 1, dc, t2)])
        src = xsrc if fh == 0 else xdst
        P.dma("sync", xs[n % 4], src[dc, :, tok], inc=s_xl)

    n = 0
    nw = 0
    NP = NTH * NFH
    s_kq = P.sem(tag + "kq")
    passes = [(th, fh) for th in range(NTH) for fh in range(NFH)]
    NSPL = 4
    qr = [((KC * k) // NSPL, (KC * (k + 1)) // NSPL) for k in range(NSPL)]

    def load_act(p, q):
        th, fh = passes[p]
        av = actT.rearrange("c p t -> p c t")
        c0, c1 = qr[q]
        P.dma("sync", Rv[:, c0:c1, :], av[:, fh * KC + c0:fh * KC + c1, th * TH:(th + 1) * TH], inc=s_al)

    for p, (th, fh) in enumerate(passes):
        al_val = 0
        if actT is not None:
            if p == 0:
                for q in range(NSPL):
                    load_act(0, q)
            al_val = 16 * NSPL * (p + 1)
        if p == 0:
            for k in range(3):
                issue_xload(k)
        for dc in range(DC):
            wbuf = wb[nw % 2]
            P.wait("gpsimd", s_m, NT2 * (nw - 1))
            P.dma("gpsimd", wbuf, wd[fh, dc], inc=s_w)
            wv = wbuf.rearrange("p (c j) -> p c j", c=KC)
            inter = (actT is not None and dc == DC - 1 and p < NP - 1)
            P.wait("tensor", s_w, 16 * (nw + 1))
            if dc == 0:
                P.wait("tensor", s_al, al_val)
            for t2 in range(NT2):
                P.wait("tensor", s_dv, n + t2 - 7)
            if not inter:
                for t2 in range(NT2):
                    ps = C.PS[(n + t2) % 8]
                    tok = slice(t2 * 512, (t2 + 1) * 512)
                    for k in range(KC):
                        fn = lambda e, k=k, ps=ps, wv=wv, tok=tok: e.matmul(
                            ps[:], lhsT=wv[:, k, :], rhs=Rv[:, k, tok], start=(k == 0), stop=(k == KC - 1))
                        P.op("tensor", fn, inc=(s_m if k == KC - 1 else None))
            else:
                marks = {qr[q][1] - 1: q for q in range(NSPL - 1)}
                for k in range(KC):
                    for t2 in range(NT2):
                        ps = C.PS[(n + t2) % 8]
                        tok = slice(t2 * 512, (t2 + 1) * 512)
                        fn = lambda e, k=k, ps=ps, wv=wv, tok=tok: e.matmul(
                            ps[:], lhsT=wv[:, k, :], rhs=Rv[:, k, tok], start=(k == 0), stop=(k == KC - 1))
                        if k == KC - 1:
                            inc = s_m
                        elif t2 == NT2 - 1 and k in marks:
                            inc = s_kq
                        else:
                            inc = None
                        P.op("tensor", fn, inc=inc)
                for q in range(NSPL - 1):
                    P.wait("sync", s_kq, (NSPL - 1) * p + q + 1)
                    load_act(p + 1, q)
            for t2 in range(NT2):
                ps = C.PS[n % 8]
                P.wait("vector", s_m, n + 1)
                P.wait("vector", s_xl, 16 * (n + 1))
                P.op("vector", lambda e, n=n, ps=ps: e.tensor_tensor(
                    out=xs[n % 4], in0=xs[n % 4], in1=ps[:], op=ALU.add), inc=s_dv)
                if n + 3 < NG:
                    issue_xload(n + 3)
                if inter and t2 == NT2 - 1:
                    P.wait("sync", s_m, n + 1)
                    load_act(p + 1, NSPL - 1)
                P.wait("sync", s_dv, n + 1)
                tokg = slice(th * TH + t2 * 512, th * TH + (t2 + 1) * 512)
                store_val[(th, fh, dc, t2)] = P.dma("sync", xdst[dc, :, tokg], xs[n % 4], inc=s_st)
                n += 1
            nw += 1
    return (s_st, s_st.n)


def phase_tm(P, C, w, vout, mode, ssqp, tag):
    hT = C.R[:, :].rearrange("p (c t) -> p c t", c=DC)
    wb = [C.WA[:, 0:8192], C.WA[:, 8192:16384]]
    vb = [C.BA[:, k * 256:(k + 1) * 256] for k in range(4)]
    junk = C.FA[:, 0:256]
    vov = vout.rearrange("g p t c -> p g t c")
    s_w, s_m, s_ev, s_q, s_st = (P.sem(tag + n) for n in ("w", "m", "ev", "q", "st"))
    for fb in range(16):
        wbuf = wb[fb % 2]
        P.wait("gpsimd", s_m, 16 * (fb - 1))
        P.dma("gpsimd", wbuf, w[fb], inc=s_w)
        wv = wbuf.rearrange("p (c j) -> p c j", c=DC)
        for tb in range(NBLK):
            n = fb * 16 + tb
            ps = C.PS[n % 8][:, 0:256]
            if tb == 0:
                P.wait("tensor", s_w, 16 * (fb + 1))
            P.wait("tensor", s_ev, n - 7)
            for c in range(DC):
                fn = lambda e, c=c, ps=ps, wv=wv, tb=tb: e.matmul(
                    ps, lhsT=hT[:, c, tb * 128:(tb + 1) * 128], rhs=wv[:, c, :],
                    start=(c == 0), stop=(c == DC - 1))
                P.op("tensor", fn, inc=(s_m if c == DC - 1 else None))
            P.wait("scalar", s_m, n + 1)
            P.wait("scalar", s_st, 16 * (n - 3))
            if mode == "gelu":
                P.op("scalar", lambda e, n=n, ps=ps: e.activation(out=vb[n % 4], in_=ps, func=AF.Gelu),
                     inc=s_ev)
                P.wait("scalar", s_ev, n + 1)
                P.op("scalar", lambda e, n=n, tb=tb, fb=fb: e.activation(
                    out=junk, in_=vb[n % 4], func=AF.Square, accum_out=ssqp[:, tb, fb:fb + 1]), inc=s_q)
                P.wait("sync", s_q, n + 1)
            else:
                P.op("scalar", lambda e, n=n, ps=ps: e.activation(out=vb[n % 4], in_=ps, func=AF.Copy),
                     inc=s_ev)
                P.wait("sync", s_ev, n + 1)
            P.dma("sync", vov[:, 2 * fb:2 * fb + 2, tb, :],
                  vb[n % 4].rearrange("p (g c) -> p g c", g=2), inc=s_st)
    return (s_st, s_st.n)


def phase_gmlp_gate(P, C, wu, wsT, bs, vtm, gT, gvcol, ssqp, tag):
    hT = C.R[:, :].rearrange("p (c t) -> p c t", c=DC)
    ub = [C.WA[:, 0:4096], C.WA[:, 4096:8192]]
    vn = [C.WA[:, 8192:10240].rearrange("p (t c) -> p t c", t=NBLK),
          C.WA[:, 10240:12288].rearrange("p (t c) -> p t c", t=NBLK)]
    gb = [C.WA[:, 12288:14336], C.WA[:, 14336:16384]]
    wsb = C.BA[:, 0:4096].rearrange("p (g t) -> p g t", g=32)
    bsb = C.FA[:, 0:4096].rearrange("p (g t) -> p g t", g=32)
    ug = [C.FA[:, 4096:4608], C.FA[:, 4608:5120]]
    sv = [C.FA[:, 5120:5632], C.FA[:, 5632:6144]]
    rstd = C.rs[:, 0, 0:16]
    s_i, s_w, s_vl, s_vn, s_sv, s_u, s_a, s_d1, s_ev, s_st = (
        P.sem(tag + n) for n in ("i", "w", "vl", "vn", "sv", "u", "a", "d1", "ev", "st"))
    P.dma("gpsimd", C.BA[:, 0:4096], wsT, inc=s_i)
    P.dma("sync", bsb, bass.AP(bs.tensor, 0, [[0, 128], [128, 32], [1, 128]]), inc=s_i)
    P.wait("vector", s_i, 32)
    P.op("vector", lambda e: e.memset(wsb[64:128, :, 0:64], 0.0), inc=s_i)
    P.op("vector", lambda e: e.tensor_reduce(out=rstd, in_=ssqp[:, :, :], axis=mybir.AxisListType.X,
                                             op=ALU.add), inc=s_i)
    P.wait("scalar", s_i, 34)
    P.op("scalar", lambda e: e.activation(out=rstd, in_=rstd, func=AF.Sqrt, bias=C.epsb[:], scale=1.0 / D),
         inc=s_i)
    P.wait("vector", s_i, 35)
    P.op("vector", lambda e: e.reciprocal(out=rstd, in_=rstd), inc=s_i)
    P.wait("vector", s_i, 36)
    P.wait("tensor", s_i, 36)
    rb = rstd.unsqueeze(2).broadcast_to([128, NBLK, 128])
    for g in range(32):
        P.wait("gpsimd", s_u, 4 * (g - 1))
        P.dma("gpsimd", ub[g % 2], wu[g], inc=s_w)
        uv = ub[g % 2].rearrange("p (c j) -> p c j", c=DC)
        if g == 0:
            P.dma("sync", vn[0], vtm[0], inc=s_vl)
        P.wait("vector", s_vl, 16 * (g + 1))
        P.op("vector", lambda e, g=g: e.tensor_tensor(out=vn[g % 2], in0=vn[g % 2], in1=rb, op=ALU.mult),
             inc=s_vn)
        for tt in range(4):
            n = g * 4 + tt
            psv = C.PS[(n % 4) * 2]
            pu = C.PS[(n % 4) * 2 + 1]
            tok = slice(tt * 512, (tt + 1) * 512)
            if tt == 0:
                P.wait("tensor", s_vn, g + 1)
                P.wait("tensor", s_w, 16 * (g + 1))
            P.wait("tensor", s_ev, n - 3)
            for k in range(4):
                tb = tt * 4 + k
                fn = lambda e, k=k, tb=tb, g=g, psv=psv: e.matmul(
                    psv[:, k * 128:(k + 1) * 128], lhsT=vn[g % 2][:, tb, :], rhs=wsb[:, g, :],
                    start=True, stop=True)
                P.op("tensor", fn, inc=(s_sv if k == 3 else None))
            for c in range(DC):
                fn = lambda e, c=c, pu=pu, uv=uv, tok=tok: e.matmul(
                    pu[:], lhsT=uv[:, c, :], rhs=hT[:, c, tok], start=(c == 0), stop=(c == DC - 1))
                P.op("tensor", fn, inc=(s_u if c == DC - 1 else None))
            P.wait("scalar", s_u, n + 1)
            P.wait("scalar", s_ev, n - 1)
            P.op("scalar", lambda e, n=n, pu=pu: e.activation(out=ug[n % 2], in_=pu[:], func=AF.Gelu),
                 inc=s_a)
            P.wait("vector", s_sv, n + 1)
            bb = bsb[:, g, :].unsqueeze(1).broadcast_to([128, 4, 128])
            P.op("vector", lambda e, n=n, g=g, psv=psv, bb=bb: e.scalar_tensor_tensor(
                out=sv[n % 2].rearrange("p (k t) -> p k t", k=4),
                in0=psv[:, :].rearrange("p (k t) -> p k t", k=4),
                scalar=C.gains[:, gvcol + g:gvcol + g + 1], in1=bb, op0=ALU.mult, op1=ALU.add), inc=s_d1)
            P.wait("vector", s_d1, n + 1)
            P.wait("vector", s_a, n + 1)
            if tt == 0:
                P.wait("vector", s_st, 16 * (g - 1))
            P.op("vector", lambda e, n=n, g=g, tok=tok: e.tensor_tensor(
                out=gb[g % 2][:, tok], in0=ug[n % 2], in1=sv[n % 2], op=ALU.mult), inc=s_ev)
        if g + 1 < 32:
            P.wait("sync", s_sv, 4 * g)
            P.dma("sync", vn[(g + 1) % 2], vtm[g + 1], inc=s_vl)
        P.wait("sync", s_ev, 4 * (g + 1))
        P.dma("sync", gT[g], gb[g % 2], inc=s_st)
    return (s_st, s_st.n)


def phase_qk(P, C, wqk, qT, kT, gq_col, gk_col, tag):
    hT = C.R[:, :].rearrange("p (c t) -> p c t", c=DC)
    wb = [C.WA[:, 0:4096], C.WA[:, 4096:8192]]
    ob = [C.WA[:, 8192:10240], C.WA[:, 10240:12288]]
    sq = [C.BA[:, 0:512], C.BA[:, 512:1024]]
    rt = [C.FA[:, 0:512], C.FA[:, 512:1024]]
    s_w, s_m, s_sq, s_o, s_rt, s_rc, s_ev, s_st = (
        P.sem(tag + n) for n in ("w", "m", "sq", "o", "rt", "rc", "ev", "st"))
    NB = 64
    NG = NB * 4

    def tail(n):
        blk, tt = divmod(n, 4)
        pm = C.PS[(n % 4) * 2]
        pq = C.PS[(n % 4) * 2 + 1]
        tok = slice(tt * 512, (tt + 1) * 512)
        col = (gq_col if blk < 32 else gk_col)
        P.wait("tensor", s_sq, n + 1)
        P.op("tensor", lambda e: e.matmul(pq[:], lhsT=C.ones128[:], rhs=sq[n % 2], start=True, stop=True),
             inc=s_o)
        P.wait("scalar", s_o, n + 1)
        P.op("scalar", lambda e: e.activation(out=rt[n % 2], in_=pq[:], func=AF.Sqrt, bias=C.epsb[:],
                                              scale=1.0), inc=s_rt)
        P.wait("vector", s_rt, n + 1)
        P.op("vector", lambda e: e.reciprocal(out=rt[n % 2], in_=rt[n % 2]), inc=s_rc)
        P.wait("vector", s_rc, n + 1)
        if tt == 0:
            P.wait("vector", s_st, 16 * (blk - 1))
        P.op("vector", lambda e: e.scalar_tensor_tensor(
            out=ob[blk % 2][:, tok], in0=pm[:], scalar=C.gains[:, col:col + 1], in1=rt[n % 2],
            op0=ALU.mult, op1=ALU.mult), inc=s_ev)
        if tt == 3:
            P.wait("sync", s_ev, 4 * (blk + 1))
            dst = qT[blk] if blk < 32 else kT[blk - 32]
            P.dma("sync", dst, ob[blk % 2], inc=s_st)

    for blk in range(NB):
        P.wait("gpsimd", s_m, 4 * (blk - 1))
        P.dma("gpsimd", wb[blk % 2], wqk[blk], inc=s_w)
        wv = wb[blk % 2].rearrange("p (c j) -> p c j", c=DC)
        for tt in range(4):
            n = blk * 4 + tt
            pm = C.PS[(n % 4) * 2]
            tok = slice(tt * 512, (tt + 1) * 512)
            if tt == 0:
                P.wait("tensor", s_w, 16 * (blk + 1))
            P.wait("tensor", s_ev, n - 3)
            for c in range(DC):
                fn = lambda e, c=c, pm=pm, wv=wv, tok=tok: e.matmul(
                    pm[:], lhsT=wv[:, c, :], rhs=hT[:, c, tok], start=(c == 0), stop=(c == DC - 1))
                P.op("tensor", fn, inc=(s_m if c == DC - 1 else None))
            P.wait("scalar", s_m, n + 1)
            P.wait("scalar", s_o, n - 1)
            P.op("scalar", lambda e, n=n, pm=pm: e.activation(out=sq[n % 2], in_=pm[:], func=AF.Square),
                 inc=s_sq)
            if n >= 1:
                tail(n - 1)
    tail(NG - 1)
    return (s_st, s_st.n)


def attn_iters():
    its = []
    for h in range(32):
        for j in range(4):
            for kb in range(8 * j + 7, -1, -1):
                its.append((h, j, kb))
    return its


def phase_attn(P, C, qT, kTg, vg, masks_sb, tag):
    oT = C.R[:, :].rearrange("p (c t) -> p c t", c=DC)
    kb_ = [C.WA[:, 0:4096].rearrange("p (r t) -> p r t", r=2),
           C.WA[:, 8192:12288].rearrange("p (r t) -> p r t", r=2)]
    vb_ = [C.WA[:, 4096:8192].rearrange("p (r i d) -> p r i d", r=2, i=NBLK),
           C.WA[:, 12288:16384].rearrange("p (r i d) -> p r i d", r=2, i=NBLK)]
    qb_ = [C.BA[:, 0:2048], C.BA[:, 2048:4096]]
    NE = 4
    eb = [C.FA[:, k * 512:(k + 1) * 512] for k in range(NE)]
    af = [C.FA[:, 2048 + k * 512:2048 + (k + 1) * 512] for k in range(2)]
    wk = C.FA[:, 6144:7424].bitcast(BF16)
    spb = [wk[:, k * 512:(k + 1) * 512] for k in range(3)]
    ab = [wk[:, 1536 + k * 512:1536 + (k + 1) * 512] for k in range(2)]
    wk2 = C.FA[:, 7424:7936].bitcast(BF16)
    acc = [wk2[:, k * 512:(k + 1) * 512] for k in range(2)]
    PZ = [C.PS[0], C.PS[1], C.PS[2]]
    PL = [C.PS[3], C.PS[4]]
    PO = [C.PS[5], C.PS[6]]
    s_hl, s_z, s_e, s_sp, s_la, s_acc, s_a, s_mul, s_av, s_oe = (
        P.sem(tag + n) for n in ("hl", "z", "e", "sp", "la", "acc", "a", "mul", "av", "oe"))
    its = attn_iters()
    N = len(its)
    kqv = [kTg[h // HPC, :, h % HPC].rearrange("r p t -> p r t") for h in range(32)]
    vv = [vg[h // HPC, :, h % HPC].rearrange("r p i d -> p r i d") for h in range(32)]
    head_end = {h: 80 * (h + 1) for h in range(32)}

    def c0_of(o):
        return {7: 3, 6: 3, 5: 2, 4: 2, 3: 1, 2: 1}.get(o, 0)

    def meta(i):
        h, j, kb = its[i]
        o = kb - 8 * j
        first = (o == 7)
        last = (kb == 0)
        r = (kb & 1) ^ ((kb >> 1) & 1)
        il = kb >> 1
        tidx = h * 4 + j
        c0 = c0_of(o) * 128
        cp = c0_of(o + 1) * 128 if not first else 512
        return h, j, kb, o, first, last, r, il, tidx, c0, cp

    def load_head(h):
        if h >= 2:
            P.wait("sync", s_av, head_end[h - 2])
        P.dma("sync", kb_[h % 2], kqv[h], inc=s_hl)
        P.dma("sync", vb_[h % 2], vv[h], inc=s_hl)
        P.dma("sync", qb_[h % 2], qT[h], inc=s_hl)

    def Z(i):
        h, j, kb, o, first, last, r, il, tidx, c0, cp = meta(i)
        if i % 80 == 0:
            P.wait("tensor", s_hl, 48 * (h + 1))
        P.wait("tensor", s_e, i - 2)
        pz = PZ[i % 3]
        hm = (o >= 0)
        P.op("tensor", lambda e: e.matmul(pz[:, c0:512], lhsT=kb_[h % 2][:, r, il * 128:(il + 1) * 128],
                                          rhs=qb_[h % 2][:, j * 512 + c0:(j + 1) * 512], start=True, stop=not hm),
             inc=(None if hm else s_z))
        if hm:
            P.op("tensor", lambda e: e.matmul(pz[:, c0:512], lhsT=C.ident[:], rhs=masks_sb[:, o, c0:512],
                                              start=False, stop=True), inc=s_z)

    def E(i):
        c0 = meta(i)[9]
        P.wait("scalar", s_z, i + 1)
        P.wait("scalar", s_mul, i - NE + 1)
        pz = PZ[i % 3]
        P.op("scalar", lambda e: e.activation(out=eb[i % NE][:, c0:512], in_=pz[:, c0:512], func=AF.Exp), inc=s_e)

    def SP(i):
        c0 = meta(i)[9]
        P.wait("scalar", s_e, i + 1)
        P.wait("scalar", s_la, i - 2)
        P.wait("scalar", s_acc, i - 2)
        P.op("scalar", lambda e: e.activation(out=spb[i % 3][:, c0:512], in_=eb[i % NE][:, c0:512], func=AF.Ln,
                                              bias=1.0, scale=1.0), inc=s_sp)

    def LA(i):
        h, j, kb, o, first, last, r, il, tidx, c0, cp = meta(i)
        P.wait("tensor", s_sp, i + 1)
        if not first:
            P.wait("tensor", s_acc, i)
        P.wait("tensor", s_a, i - 1)
        pl = PL[i % 2]
        P.op("tensor", lambda e: e.matmul(pl[:, c0:512], lhsT=C.negtri[:], rhs=spb[i % 3][:, c0:512],
                                          start=True, stop=first), inc=(s_la if first else None))
        if not first:
            P.op("tensor", lambda e: e.matmul(pl[:, cp:512], lhsT=C.negones[:], rhs=acc[tidx % 2][:, cp:512],
                                              start=False, stop=True), inc=s_la)

    def ACC(i):
        h, j, kb, o, first, last, r, il, tidx, c0, cp = meta(i)
        P.wait("vector", s_sp, i + 1)
        P.wait("vector", s_la, i + 1)
        if not first:
            P.wait("vector", s_acc, i)
        if c0 < cp:
            P.op("vector", lambda e: e.tensor_copy(out=acc[tidx % 2][:, c0:cp], in_=spb[i % 3][:, c0:cp]),
                 inc=(s_acc if first else None))
        if not first:
            P.op("vector", lambda e: e.tensor_tensor(out=acc[tidx % 2][:, cp:512], in0=acc[tidx % 2][:, cp:512],
                                                     in1=spb[i % 3][:, cp:512], op=ALU.add), inc=s_acc)

    def A(i):
        c0 = meta(i)[9]
        P.wait("scalar", s_la, i + 1)
        P.wait("scalar", s_mul, i - 1)
        pl = PL[i % 2]
        P.op("scalar", lambda e: e.activation(out=af[i % 2][:, c0:512], in_=pl[:, c0:512], func=AF.Exp), inc=s_a)

    def MUL(i):
        c0 = meta(i)[9]
        P.wait("vector", s_a, i + 1)
        P.wait("vector", s_av, i - 1)
        P.op("vector", lambda e: e.tensor_tensor(out=ab[i % 2][:, c0:512], in0=af[i % 2][:, c0:512],
                                                 in1=eb[i % NE][:, c0:512], op=ALU.mult), inc=s_mul)

    def AV(i):
        h, j, kb, o, first, last, r, il, tidx, c0, cp = meta(i)
        P.wait("tensor", s_mul, i + 1)
        if first:
            P.wait("tensor", s_oe, tidx - 1)
        po = PO[tidx % 2]
        P.op("tensor", lambda e: e.matmul(po[:, c0:512], lhsT=vb_[h % 2][:, r, il, :], rhs=ab[i % 2][:, c0:512],
                                          start=first, stop=last), inc=s_av)
        if last:
            P.wait("vector", s_av, i + 1)
            P.op("vector", lambda e: e.tensor_copy(out=oT[:, h, j * 512:(j + 1) * 512], in_=po[:]), inc=s_oe)

    load_head(0)
    load_head(1)
    for s in range(-1, N + 2):
        if 0 <= s + 1 < N:
            h = its[s + 1][0]
            if (s + 1) % 80 == 0 and h >= 1 and h + 1 < 32:
                load_head(h + 1)
            Z(s + 1)
            E(s + 1)
        if 0 <= s < N:
            SP(s)
        if 0 <= s - 1 < N:
            LA(s - 1)
            A(s - 1)
            ACC(s - 1)
            MUL(s - 1)
        if 0 <= s - 2 < N:
            AV(s - 2)
    return (s_oe, 128)


def ffn_layer(P, C, gcol, wgu, wd, actT, xsrc, xdst, tag):
    d = phase_norm(P, C, xsrc, gcol, tag + "n")
    barrier(P, [d])
    d = phase_ffn_g(P, C, wgu, actT, tag + "g")
    barrier(P, [d])
    d = phase_proj_res(P, C, wd, FH, 2, 2, actT, xsrc, xdst, tag + "d")
    barrier(P, [d])
    return d


def gmlp_layer(P, C, j, W, S, xsrc, xdst, tag):
    d = phase_norm(P, C, xsrc, GCOL["a_norm"][j], tag + "n")
    barrier(P, [d])
    d = phase_tm(P, C, W[f"a_wv{j}"], S["vtm"], "gelu", C.ssqp, tag + "v")
    barrier(P, [d])
    d = phase_gmlp_gate(P, C, W[f"a_wu{j}"], W[f"a_ws{j}"], W[f"a_bs{j}"], S["vtm"], S["gT"],
                        GCOL["a_vn"][j], C.ssqp, tag + "g")
    barrier(P, [d])
    d = phase_proj_res(P, C, W[f"a_wo{j}"], DC, 1, 1, S["gT"], xsrc, xdst, tag + "o")
    barrier(P, [d])
    return d


def gather_chunks(P, src2d, dstg, pat, sem):
    rows = 4096 // NCH
    for ck in range(NCH):
        def fn(e, ck=ck):
            return e.collective_compute("AllGather", ALU.bypass, replica_groups=PAIRS,
                                        ins=[src2d[ck * rows:(ck + 1) * rows, :].opt()],
                                        outs=[dstg[ck].rearrange(pat).opt()])
        P.op("gpsimd", fn, inc=sem)


def bproj_layer(P, C, j, W, S, xsrc, qT, kT, vl, tag, gather=None):
    d = phase_norm(P, C, xsrc, GCOL["b_norm"][j], tag + "n")
    barrier(P, [d])
    d = phase_qk(P, C, W[f"b_wqk{j}"], qT, kT, GCOL["q"][j], GCOL["k"][j], tag + "q")
    barrier(P, [d])
    d = phase_tm(P, C, W[f"b_wv{j}"], vl, "copy", None, tag + "v")
    barrier(P, [d])
    if gather is not None:
        kTg, vg = gather
        sg = P.sem(tag + "cc")
        gather_chunks(P, kT.rearrange("h p t -> (h p) t"), kTg, "r h p t -> (r h p) t", sg)
        gather_chunks(P, vl.rearrange("h p i d -> (h p) (i d)"), vg, "r h p i d -> (r h p) (i d)", sg)
        d = (sg, sg.n)
        barrier(P, [d])
    return d


def battn_layer(P, C, j, W, S, masks, qT, kTg, vg, xsrc, xdst, tag):
    msb = C.FA[:, 4096:6144].bitcast(BF16).rearrange("p (o t) -> p o t", o=8)
    sm = P.sem(tag + "mk")
    P.dma("gpsimd", C.FA[:, 4096:6144].bitcast(BF16), masks, inc=sm)
    barrier(P, [(sm, 16)])
    d = phase_attn(P, C, qT, kTg, vg, msb, tag + "a")
    barrier(P, [d])
    d = phase_proj_res(P, C, W[f"b_wo{j}"], DC, 1, 1, None, xsrc, xdst, tag + "o")
    barrier(P, [d])
    return d


WSHAPES = {}
for _j in range(2):
    WSHAPES[f"a_wv{_j}"] = [16, 128, 8192]
    WSHAPES[f"a_wu{_j}"] = [32, 128, 4096]
    WSHAPES[f"a_ws{_j}"] = [128, 4096]
    WSHAPES[f"a_bs{_j}"] = [32, 128]
    WSHAPES[f"a_wo{_j}"] = [1, 32, 128, 4096]
    WSHAPES[f"b_wqk{_j}"] = [64, 128, 4096]
    WSHAPES[f"b_wv{_j}"] = [16, 128, 8192]
    WSHAPES[f"b_wo{_j}"] = [1, 32, 128, 4096]
for _l in range(4):
    WSHAPES[f"wgu{_l}"] = [FC, 128, 8192]
    WSHAPES[f"wd{_l}"] = [2, DC, 128, FH * 128]

SEG_STEPS = {
    0: ["A0", "F0", "P0"],
    1: ["T0", "F1", "A1", "F2", "P1"],
    2: ["T1", "F3"],
    "t": ["T0"],
    "all": ["A0", "F0", "P0", "G0", "T0", "F1", "A1", "F2", "P1", "G1", "T1", "F3"],
}
STEP_W = {"A0": ["a_wv0", "a_wu0", "a_ws0", "a_bs0", "a_wo0"], "A1": ["a_wv1", "a_wu1", "a_ws1", "a_bs1", "a_wo1"],
          "P0": ["b_wqk0", "b_wv0"], "P1": ["b_wqk1", "b_wv1"], "T0": ["b_wo0"], "T1": ["b_wo1"],
          "F0": ["wgu0", "wd0"], "F1": ["wgu1", "wd1"], "F2": ["wgu2", "wd2"], "F3": ["wgu3", "wd3"],
          "G0": [], "G1": []}
PAIRS = [[0, 1], [2, 3], [4, 5], [6, 7]]


def seg_weight_names(seg):
    out = []
    for st in SEG_STEPS[seg]:
        out += STEP_W[st]
    return out


def build_program(seg):
    nc = bass.Bass("TRN2", target_bir_lowering=False)
    steps = SEG_STEPS[seg]
    fused = (seg == "all")
    W = {}
    for name in seg_weight_names(seg):
        W[name] = nc.dram_tensor(name, WSHAPES[name], F32, kind="ExternalInput").ap()
    gains = nc.dram_tensor("gains", [128, NGAIN], F32, kind="ExternalInput").ap()
    consts = nc.dram_tensor("consts", [128, 3, 128], F32, kind="ExternalInput").ap()
    masks = nc.dram_tensor("masks", [128, 4096], F32, kind="ExternalInput").ap()
    xin = nc.dram_tensor("xin", [DC, 128, T], F32, kind="ExternalInput").ap()
    xo = nc.dram_tensor("xo", [DC, 128, T], F32, kind="ExternalOutput").ap()
    S = {}
    S["actT"] = nc.dram_tensor("actT", [FC, 128, T], BF16, kind="Internal").ap()
    S["vtm"] = nc.dram_tensor("vtm", [32, 128, NBLK, 128], BF16, kind="Internal").ap()
    S["gT"] = nc.dram_tensor("gT", [32, 128, T], BF16, kind="Internal").ap()
    if fused:
        qT = nc.dram_tensor("qT", [32, 128, T], BF16, kind="Internal").ap()
        kT = nc.dram_tensor("kT", [32, 128, T], BF16, kind="Internal").ap()
        vl = nc.dram_tensor("vl", [32, 128, NBLK, 128], BF16, kind="Internal").ap()
        kTgs = [nc.dram_tensor(f"kTg{i}", [NCH, 2, HPC, 128, T], BF16, kind="Internal").ap() for i in range(2)]
        vgs = [nc.dram_tensor(f"vg{i}", [NCH, 2, HPC, 128, NBLK, 128], BF16, kind="Internal").ap() for i in range(2)]
        q_in = qT
    else:
        if seg in (0, 1):
            qT = nc.dram_tensor("qTo", [32, 128, T], BF16, kind="ExternalOutput").ap()
            kT = nc.dram_tensor("kTo", [32, 128, T], BF16, kind="ExternalOutput").ap()
            vl = nc.dram_tensor("vlo", [32, 128, NBLK, 128], BF16, kind="ExternalOutput").ap()
        if seg in (1, 2, "t"):
            q_in = nc.dram_tensor("qin", [32, 128, T], BF16, kind="ExternalInput").ap()
            kTg = nc.dram_tensor("kTg", [NCH, 2, HPC, 128, T], BF16, kind="ExternalInput").ap()
            vg = nc.dram_tensor("vg", [NCH, 2, HPC, 128, NBLK, 128], BF16, kind="ExternalInput").ap()
    with ExitStack() as st:
        P = Prog(nc, st)
        C = setup_ctx(P, gains[:, :], NGAIN, consts[:, :, :])
        xcur = xin
        d = None
        for step in steps:
            kind, idx = step[0], int(step[1])
            tag = step.lower()
            if kind == "A":
                d = gmlp_layer(P, C, idx, W, S, xcur, xo, tag)
                xcur = xo
            elif kind == "F":
                d = ffn_layer(P, C, GCOL["f_norm"][idx], W[f"wgu{idx}"], W[f"wd{idx}"], S["actT"], xcur, xo, tag)
                xcur = xo
            elif kind == "P":
                gth = (kTgs[idx], vgs[idx]) if fused else None
                d = bproj_layer(P, C, idx, W, S, xcur, qT, kT, vl, tag, gather=gth)
            elif kind == "G":
                pass
            elif kind == "T":
                if fused:
                    kTg, vg = kTgs[idx], vgs[idx]
                d = battn_layer(P, C, idx, W, S, masks, q_in, kTg, vg, xcur, xo, tag)
                xcur = xo
        barrier(P, [d])
        P.emit()
    return nc


def host_layout_ffn(w_gate, w_up, w_down):
    g = w_gate.reshape(DC, 128, FC, 128).transpose(2, 1, 0, 3)
    u = w_up.reshape(DC, 128, FC, 128).transpose(2, 1, 0, 3)
    wgu = np.empty((FC, 128, 2, DC, 128), np.float32)
    wgu[:, :, 0] = g
    wgu[:, :, 1] = u
    wgu = wgu.reshape(FC, 128, 2 * DC * 128)
    wd = np.ascontiguousarray(
        w_down.reshape(2, FH, 128, DC, 128).transpose(0, 3, 2, 1, 4)).reshape(2, DC, 128, FH * 128)
    return wgu, wd


def gain_cols(g):
    return np.ascontiguousarray(g.reshape(DC, 128).T)


def block_perm(r):
    return [2 * i + (r ^ (i & 1)) for i in range(NBLK)]


def x_to_core(x):
    outs = []
    for c in range(NCORES):
        b, r = c // 2, c % 2
        xb = x[b].reshape(SEQ // 128, 128, D)[block_perm(r)].reshape(T, D)
        outs.append(np.ascontiguousarray(xb.T).reshape(DC, 128, T))
    return outs


def core_to_x(outs):
    y = np.empty((BATCH, SEQ, D), np.float32)
    for c in range(NCORES):
        b, r = c // 2, c % 2
        xt = outs[c].reshape(D, T).T.reshape(NBLK, 128, D)
        yb = y[b].reshape(SEQ // 128, 128, D)
        yb[block_perm(r)] = xt
    return y


def tile_cols(w, nblk, width):
    return np.ascontiguousarray(
        w.reshape(DC, 128, nblk, width).transpose(2, 1, 0, 3)).reshape(nblk, 128, DC * width)


def host_weights(inp, names):
    Wd = {}
    for name in names:
        base, j = name[:-1], int(name[-1])
        if base == "a_wv":
            Wd[name] = tile_cols(inp["a_w_in"][j][:, 4096:], 16, 256)
        elif base == "a_wu":
            Wd[name] = tile_cols(inp["a_w_in"][j][:, :4096], 32, 128)
        elif base == "a_ws":
            Wd[name] = np.ascontiguousarray(inp["a_w_s"][j].transpose(2, 0, 1)).reshape(128, 4096)
        elif base == "a_bs":
            Wd[name] = np.ascontiguousarray(inp["a_b_s"][j])
        elif base == "a_wo":
            Wd[name] = tile_cols(inp["a_w_out"][j], 32, 128).reshape(1, 32, 128, 4096)
        elif base == "b_wqk":
            Wd[name] = tile_cols(inp["b_w_qkv"][j][:, :8192], 64, 128)
        elif base == "b_wv":
            Wd[name] = tile_cols(inp["b_w_qkv"][j][:, 8192:], 16, 256)
        elif base == "b_wo":
            Wd[name] = tile_cols(inp["b_w_out"][j], 32, 128).reshape(1, 32, 128, 4096)
        elif base == "wgu":
            Wd[name], Wd[f"wd{j}"] = host_layout_ffn(inp["f_w_gate"][j], inp["f_w_up"][j], inp["f_w_down"][j])
        elif base == "wd":
            pass
        else:
            raise KeyError(name)
    return Wd


def host_gains(inp):
    g = np.zeros((128, NGAIN), np.float32)
    for j in range(2):
        g[:, GCOL["a_norm"][j]:GCOL["a_norm"][j] + 32] = gain_cols(inp["a_norm"][j])
        g[:, GCOL["a_vn"][j]:GCOL["a_vn"][j] + 32] = gain_cols(inp["a_v_norm"][j])
        g[:, GCOL["b_norm"][j]:GCOL["b_norm"][j] + 32] = gain_cols(inp["b_norm"][j])
        g[:, GCOL["q"][j]] = inp["b_q_norm"][j]
        g[:, GCOL["k"][j]] = inp["b_k_norm"][j]
    for i in range(4):
        g[:, GCOL["f_norm"][i]:GCOL["f_norm"][i] + 32] = gain_cols(inp["f_norm"][i])
    return g


def host_consts():
    c = np.zeros((128, 3, 128), np.float32)
    c[:, 0, :] = np.eye(128, dtype=np.float32)
    jj, ss = np.meshgrid(np.arange(128), np.arange(128), indexing="ij")
    c[:, 1, :] = np.where(jj >= ss, -1.0, 0.0)
    c[:, 2, :] = -1.0
    return c


def host_masks(r):
    rel = [i * 2 + (r ^ (i & 1)) for i in range(4)]
    m = np.zeros((128, 8, 4, 128), np.float32)
    ss, tt = np.meshgrid(np.arange(128), np.arange(128), indexing="ij")
    diag = np.where(ss >= tt, NEG, 0.0).astype(np.float32)
    for o in range(8):
        for i in range(4):
            if rel[i] == o:
                m[:, o, i, :] = diag
            elif rel[i] < o:
                m[:, o, i, :] = NEG
    return m.reshape(128, 4096)


_PROG_CACHE = {}


def _get_prog(seg):
    if seg not in _PROG_CACHE:
        _PROG_CACHE[seg] = build_program(seg)
    return _PROG_CACHE[seg]


FUSED = True


def _pair_gather(arrs):
    out = []
    for c in range(NCORES):
        p = (c // 2) * 2
        a = arrs[p].reshape((NCH, HPC) + arrs[p].shape[1:])
        b = arrs[p + 1].reshape((NCH, HPC) + arrs[p + 1].shape[1:])
        out.append(np.stack([a, b], axis=1))
    return out


def kernel(**inputs):
    inp = {k: np.asarray(v) for k, v in inputs.items()}
    xs = x_to_core(inp["x"].astype(np.float32, copy=False))
    gains = host_gains(inp)
    consts = host_consts()
    masks = [host_masks(c % 2) for c in range(NCORES)]
    cores = list(range(NCORES))
    if FUSED:
        Wd = host_weights(inp, seg_weight_names("all"))
        ims = []
        for c in cores:
            im = {"xin": xs[c], "gains": gains, "consts": consts, "masks": masks[c]}
            im.update(Wd)
            ims.append(im)
        res = run_bass_kernel_spmd(_get_prog("all"), ims, core_ids=cores)
        return core_to_x([res.results[c]["xo"] for c in cores])
    xcur = xs
    q = kg = vg = None
    for seg in (0, 1, 2):
        Wd = host_weights(inp, seg_weight_names(seg))
        ims = []
        for c in cores:
            im = {"xin": xcur[c], "gains": gains, "consts": consts, "masks": masks[c]}
            if seg >= 1:
                im.update({"qin": q[c], "kTg": kg[c], "vg": vg[c]})
            im.update(Wd)
            ims.append(im)
        res = run_bass_kernel_spmd(_get_prog(seg), ims, core_ids=cores)
        del ims, Wd
        xcur = [res.results[c]["xo"] for c in cores]
        if seg < 2:
            q = [res.results[c]["qTo"] for c in cores]
            kg = _pair_gather([res.results[c]["kTo"] for c in cores])
            vg = _pair_gather([res.results[c]["vlo"] for c in cores])
    return core_to_x(xcur)
```
